# Optimizing a Trainium2 kernel written in Bass

```python
import math
import jax, jax.numpy as jnp
from jax import lax
import numpy as np

D_MODEL = 1024
BATCH = 4
SEQ = 4096
DEPTH = 2

N_A_LAYERS = DEPTH // 2
N_B_LAYERS = DEPTH - N_A_LAYERS
S5_GROUP = 16
S5_GROUPS = D_MODEL // S5_GROUP
S5_STATE = 64
DT_MIN = 0.001
DT_MAX = 0.1
SB_HEADS = 16
SB_HEAD_DIM = D_MODEL // SB_HEADS
Q_BLOCK = 128
D_FF = 4 * D_MODEL
DEEPNORM_ALPHA = (2.0 * DEPTH) ** 0.25
DEEPNORM_BETA = (8.0 * DEPTH) ** -0.25
LN_EPS = 1e-5

kernel_name = "s5_stickbreaking_yoco_deepnorm"


def layer_norm(x, g, b):
    xf = x.astype(jnp.float32)
    mu = jnp.mean(xf, axis=-1, keepdims=True)
    var = jnp.mean(jnp.square(xf - mu), axis=-1, keepdims=True)
    y = (xf - mu) * lax.rsqrt(var + LN_EPS)
    return (y * g.astype(jnp.float32) + b.astype(jnp.float32)).astype(x.dtype)


def s5_mixer(x, w_in, lam_re, lam_im, b_re, b_im, c_re, c_im, d_skip, log_step, w_glu, b_glu, w_out):
    bsz, seq, _ = x.shape
    u = (x @ w_in).astype(jnp.float32).reshape(bsz, seq, S5_GROUPS, S5_GROUP)
    dt = jnp.exp(log_step.astype(jnp.float32))[:, None]
    lr = lam_re.astype(jnp.float32)
    li = lam_im.astype(jnp.float32)
    mag = jnp.exp(lr * dt)
    ang = li * dt
    ab_re = mag * jnp.cos(ang)
    ab_im = mag * jnp.sin(ang)
    nr = ab_re - 1.0
    ni = ab_im
    den = lr * lr + li * li
    f_re = (nr * lr + ni * li) / den
    f_im = (ni * lr - nr * li) / den
    br = b_re.astype(jnp.float32)
    bi = b_im.astype(jnp.float32)
    bb_re = f_re[..., None] * br - f_im[..., None] * bi
    bb_im = f_re[..., None] * bi + f_im[..., None] * br
    bu_re = jnp.einsum('blgh,gph->lbgp', u, bb_re)
    bu_im = jnp.einsum('blgh,gph->lbgp', u, bb_im)
    a_re = jnp.broadcast_to(ab_re[None, None], (seq, 1, S5_GROUPS, S5_STATE))
    a_im = jnp.broadcast_to(ab_im[None, None], (seq, 1, S5_GROUPS, S5_STATE))

    def combine(e_i, e_j):
        ar_i, ai_i, sr_i, si_i = e_i
        ar_j, ai_j, sr_j, si_j = e_j
        return (ar_j * ar_i - ai_j * ai_i,
                ar_j * ai_i + ai_j * ar_i,
                ar_j * sr_i - ai_j * si_i + sr_j,
                ar_j * si_i + ai_j * sr_i + si_j)

    _, _, s_re, s_im = lax.associative_scan(combine, (a_re, a_im, bu_re, bu_im), axis=0)
    y = (jnp.einsum('lbgp,ghp->blgh', s_re, c_re.astype(jnp.float32))
         - jnp.einsum('lbgp,ghp->blgh', s_im, c_im.astype(jnp.float32)))
    y = (y + d_skip.astype(jnp.float32).reshape(S5_GROUPS, S5_GROUP) * u).reshape(bsz, seq, D_MODEL)
    g = jax.nn.gelu(y)
    out = g * jax.nn.sigmoid(g @ w_glu.astype(jnp.float32) + b_glu.astype(jnp.float32))
    return out.astype(x.dtype) @ w_out


def shared_kv(x, w_kv):
    bsz, seq, _ = x.shape
    kv = x @ w_kv
    k, v = jnp.split(kv, 2, axis=-1)
    k = k.reshape(bsz, seq, SB_HEADS, SB_HEAD_DIM).transpose(0, 2, 1, 3)
    v = v.reshape(bsz, seq, SB_HEADS, SB_HEAD_DIM).transpose(0, 2, 1, 3)
    return k, v


def stick_breaking_attention(x, k, v, w_q, w_out):
    bsz, seq, _ = x.shape
    q = (x @ w_q).reshape(bsz, seq, SB_HEADS, SB_HEAD_DIM).transpose(0, 2, 1, 3)
    scale = 1.0 / math.sqrt(SB_HEAD_DIM)
    outs = []
    for blk in range(seq // Q_BLOCK):
        start = blk * Q_BLOCK
        end = start + Q_BLOCK
        qb = q[:, :, start:end]
        kb = k[:, :, :end]
        vb = v[:, :, :end]
        z = jnp.einsum('bhqd,bhkd->bhqk', qb, kb, preferred_element_type=jnp.float32) * scale
        t_pos = start + jnp.arange(Q_BLOCK)
        s_pos = jnp.arange(end)
        causal = s_pos[None, :] < t_pos[:, None]
        log_1mb = jnp.where(causal, jax.nn.log_sigmoid(-z), 0.0)
        suffix = lax.cumsum(log_1mb, axis=3, reverse=True) - log_1mb
        w = jnp.where(causal, jnp.exp(jax.nn.log_sigmoid(z) + suffix), 0.0)
        outs.append(jnp.einsum('bhqk,bhkd->bhqd', w.astype(vb.dtype), vb))
    o = jnp.concatenate(outs, axis=2).transpose(0, 2, 1, 3).reshape(bsz, seq, D_MODEL)
    return o @ w_out


def squared_relu_mlp(x, w1, b1, w2, b2):
    h = jnp.square(jax.nn.relu(x @ w1 + b1))
    return h @ w2 + b2


def setup_inputs(seed: int = 0) -> dict:
    key = jax.random.key(seed)
    ks = jax.random.split(key, 24)
    D, G, P, H = D_MODEL, S5_GROUPS, S5_STATE, S5_GROUP
    nA, nB = N_A_LAYERS, N_B_LAYERS
    nrm = jax.random.normal
    x = nrm(ks[0], (BATCH, SEQ, D), jnp.float32)
    s5_w_in = nrm(ks[1], (nA, D, D), jnp.float32) * D ** -0.5
    s5_lambda_re = -0.5 + 0.01 * nrm(ks[2], (nA, G, P), jnp.float32)
    s5_lambda_im = (jnp.pi * jnp.arange(P, dtype=jnp.float32))[None, None, :] + 0.01 * nrm(ks[3], (nA, G, P), jnp.float32)
    s5_b_re = nrm(ks[4], (nA, G, P, H), jnp.float32) * (2.0 * H) ** -0.5
    s5_b_im = nrm(ks[5], (nA, G, P, H), jnp.float32) * (2.0 * H) ** -0.5
    s5_c_re = nrm(ks[6], (nA, G, H, P), jnp.float32) * (2.0 * P) ** -0.5
    s5_c_im = nrm(ks[7], (nA, G, H, P), jnp.float32) * (2.0 * P) ** -0.5
    s5_d = nrm(ks[8], (nA, D), jnp.float32)
    s5_log_step = jax.random.uniform(ks[9], (nA, G), jnp.float32, math.log(DT_MIN), math.log(DT_MAX))
    s5_w_glu = nrm(ks[10], (nA, D, D), jnp.float32) * D ** -0.5
    s5_b_glu = 0.02 * nrm(ks[11], (nA, D), jnp.float32)
    s5_w_out = nrm(ks[12], (nA, D, D), jnp.float32) * D ** -0.5 * DEEPNORM_BETA
    w_k = nrm(ks[13], (D, D), jnp.float32) * D ** -0.5
    w_v = nrm(ks[14], (D, D), jnp.float32) * D ** -0.5 * DEEPNORM_BETA
    sb_w_kv = jnp.concatenate([w_k, w_v], axis=1)
    sb_w_q = nrm(ks[15], (nB, D, D), jnp.float32) * D ** -0.5
    sb_w_out = nrm(ks[16], (nB, D, D), jnp.float32) * D ** -0.5 * DEEPNORM_BETA
    mlp_w1 = nrm(ks[17], (DEPTH, D, D_FF), jnp.float32) * D ** -0.5
    mlp_b1 = 0.02 * nrm(ks[18], (DEPTH, D_FF), jnp.float32)
    mlp_w2 = nrm(ks[19], (DEPTH, D_FF, D), jnp.float32) * D_FF ** -0.5 * DEEPNORM_BETA
    mlp_b2 = 0.02 * nrm(ks[20], (DEPTH, D), jnp.float32)
    lk = jax.random.split(ks[21], 4)
    ln_mix_g = 1.0 + 0.05 * nrm(lk[0], (DEPTH, D), jnp.float32)
    ln_mix_b = 0.02 * nrm(lk[1], (DEPTH, D), jnp.float32)
    ln_mlp_g = 1.0 + 0.05 * nrm(lk[2], (DEPTH, D), jnp.float32)
    ln_mlp_b = 0.02 * nrm(lk[3], (DEPTH, D), jnp.float32)
    return {"x": x,
            "s5_w_in": s5_w_in, "s5_lambda_re": s5_lambda_re, "s5_lambda_im": s5_lambda_im,
            "s5_b_re": s5_b_re, "s5_b_im": s5_b_im, "s5_c_re": s5_c_re, "s5_c_im": s5_c_im,
            "s5_d": s5_d, "s5_log_step": s5_log_step, "s5_w_glu": s5_w_glu, "s5_b_glu": s5_b_glu,
            "s5_w_out": s5_w_out,
            "sb_w_kv": sb_w_kv, "sb_w_q": sb_w_q, "sb_w_out": sb_w_out,
            "mlp_w1": mlp_w1, "mlp_b1": mlp_b1, "mlp_w2": mlp_w2, "mlp_b2": mlp_b2,
            "ln_mix_g": ln_mix_g, "ln_mix_b": ln_mix_b, "ln_mlp_g": ln_mlp_g, "ln_mlp_b": ln_mlp_b}


def reference(x, s5_w_in, s5_lambda_re, s5_lambda_im, s5_b_re, s5_b_im, s5_c_re, s5_c_im,
              s5_d, s5_log_step, s5_w_glu, s5_b_glu, s5_w_out,
              sb_w_kv, sb_w_q, sb_w_out,
              mlp_w1, mlp_b1, mlp_w2, mlp_b2,
              ln_mix_g, ln_mix_b, ln_mlp_g, ln_mlp_b):
    k_shared = None
    v_shared = None
    for layer in range(DEPTH):
        if layer < N_A_LAYERS:
            a = layer
            mix = s5_mixer(x, s5_w_in[a], s5_lambda_re[a], s5_lambda_im[a], s5_b_re[a], s5_b_im[a],
                           s5_c_re[a], s5_c_im[a], s5_d[a], s5_log_step[a], s5_w_glu[a], s5_b_glu[a],
                           s5_w_out[a])
        else:
            if layer == N_A_LAYERS:
                k_shared, v_shared = shared_kv(x, sb_w_kv)
            bi = layer - N_A_LAYERS
            mix = stick_breaking_attention(x, k_shared, v_shared, sb_w_q[bi], sb_w_out[bi])
        x = layer_norm(DEEPNORM_ALPHA * x + mix, ln_mix_g[layer], ln_mix_b[layer])
        ff = squared_relu_mlp(x, mlp_w1[layer], mlp_b1[layer], mlp_w2[layer], mlp_b2[layer])
        x = layer_norm(DEEPNORM_ALPHA * x + ff, ln_mlp_g[layer], ln_mlp_b[layer])
    return x
```

```python
import contextlib
import math

import numpy as np

import concourse.bass as bass
import concourse.mybir as mybir
from concourse.bass_utils import run_bass_kernel_spmd

F32 = mybir.dt.float32
BF16 = mybir.dt.bfloat16
U32 = mybir.dt.uint32
AF = mybir.ActivationFunctionType
ALU = mybir.AluOpType
AX = mybir.AxisListType

D = 1024
DFF = 4096
SEQ = 4096
BATCH = 4
ALPHA = (2.0 * 2) ** 0.25
EPS = 1e-5

ENGS = ("pe", "act", "dve", "pool", "sp")
NDMA = 24


class _Op:
    __slots__ = ("eng", "fn", "waits", "is_dma", "needs_inc", "token", "dma_slot", "dma_round")

    def __init__(self, eng, fn, is_dma):
        self.eng = eng
        self.fn = fn
        self.is_dma = is_dma
        self.waits = []
        self.needs_inc = False
        self.token = None
        self.dma_slot = None
        self.dma_round = None


class Sched:
    def __init__(self, nc, same_engine_sync=True):
        self.nc = nc
        self.ops = {e: [] for e in ENGS}
        self.last_w = {}
        self.readers = {}
        self.dma_count = {e: 0 for e in ENGS}
        self.dma_hist = {e: [] for e in ENGS}
        self.same_engine_sync = same_engine_sync
        self.pending_barrier = {e: [] for e in ENGS}

    def barrier(self):
        tails = []
        for e in ENGS:
            comp = [o for o in self.ops[e] if not o.is_dma]
            if comp:
                tails.append(comp[-1])
            tails.extend(self.dma_hist[e][-NDMA:])
        for e in ENGS:
            self.pending_barrier[e] = list(tails)

    def op(self, eng, fn, reads=(), writes=(), dma=False):
        o = _Op(eng, fn, dma)
        deps = []
        force = self.pending_barrier[eng]
        self.pending_barrier[eng] = []
        for k in reads:
            w = self.last_w.get(k)
            if w is not None:
                deps.append(w)
        for k in writes:
            w = self.last_w.get(k)
            if w is not None:
                deps.append(w)
            deps.extend(self.readers.get(k, ()))
        if dma:
            n = self.dma_count[eng]
            o.dma_slot = n % NDMA
            o.dma_round = n // NDMA
            self.dma_count[eng] = n + 1
            hist = self.dma_hist[eng]
            if n >= NDMA:
                deps.append(hist[n - NDMA])
            hist.append(o)
            o.needs_inc = True
        seen = set()
        for d in force:
            if id(d) in seen:
                continue
            seen.add(id(d))
            if d.eng == eng and not d.is_dma:
                continue
            d.needs_inc = True
            o.waits.append(d)
        for d in deps:
            if d is o or id(d) in seen:
                continue
            seen.add(id(d))
            if d.eng == eng and not d.is_dma and not dma:
                if eng == "pe" or not self.same_engine_sync:
                    continue
            d.needs_inc = True
            o.waits.append(d)
        for k in reads:
            self.readers.setdefault(k, []).append(o)
        for k in writes:
            self.last_w[k] = o
            self.readers[k] = []
        self.ops[eng].append(o)
        return o

    def emit(self, final_wait_ops=()):
        nc = self.nc
        with contextlib.ExitStack() as st:
            esem = {e: st.enter_context(nc.semaphore(f"s_{e}")) for e in ENGS}
            dsem = {e: [st.enter_context(nc.semaphore(f"d_{e}{i}")) for i in range(NDMA)]
                    for e in ENGS if self.dma_count[e] > 0}
            for e in ENGS:
                c = 0
                for o in self.ops[e]:
                    if o.is_dma:
                        o.token = (dsem[e][o.dma_slot], 16 * (o.dma_round + 1))
                    elif o.needs_inc:
                        c += 1
                        o.token = (esem[e], c)
            block = st.enter_context(nc.Block())

            def run(e):
                def body(eng):
                    waited = {}
                    for o in self.ops[e]:
                        for d in o.waits:
                            sem, val = d.token
                            key = id(sem)
                            if waited.get(key, 0) >= val:
                                continue
                            waited[key] = val
                            eng.wait_ge(sem, val)
                        ins = o.fn(eng)
                        if o.is_dma:
                            ins.then_inc(o.token[0], 16)
                        elif o.needs_inc:
                            ins.then_inc(o.token[0], 1)
                    if e == "sp":
                        for d in final_wait_ops:
                            sem, val = d.token
                            eng.wait_ge(sem, val)
                return body

            block.tensor(run("pe"))
            block.scalar(run("act"))
            block.vector(run("dve"))
            block.gpsimd(run("pool"))
            block.sync(run("sp"))


class Ctx:
    def __init__(self, same_engine_sync=True):
        self.nc = bass.Bass("TRN2", target_bir_lowering=False)
        self.S = Sched(self.nc, same_engine_sync)
        self.st = contextlib.ExitStack()
        self.pst = contextlib.ExitStack()
        self.psn = 0
        self.banks = None
        self.outs = []
        self.ring = 8
        self.stage = 0

    def din(self, name, shape, dt=F32):
        return self.nc.dram_tensor(name, list(shape), dt, kind="ExternalInput").ap()

    def dout(self, name, shape, dt=F32):
        return self.nc.dram_tensor(name, list(shape), dt, kind="ExternalOutput").ap()

    def sb(self, name, shape, dt=F32):
        return self.st.enter_context(self.nc.sbuf_tensor(f"{name}_s{self.stage}", list(shape), dt))

    def init_psum(self):
        self.banks = [self.pst.enter_context(self.nc.psum_tensor(f"ps{i}", [128, 512], F32)) for i in range(8)]

    def psum(self):
        i = self.psn % self.ring
        self.psn += 1
        return self.banks[i], ("ps", i)

    def dma(self, q, out, in_, reads=(), writes=(), **kw):
        return self.S.op(q, lambda e: e.dma_start(out=out, in_=in_, **kw), reads=reads, writes=writes, dma=True)

    def wload(self, out, in_, writes, alt=0):
        if in_.dtype == BF16:
            return self.dma("sp" if alt % 2 == 0 else "act", out, in_, writes=writes)
        return self.dma("pool", out, in_, writes=writes)

    def mm(self, out, lhsT, rhs, start, stop, reads, writes):
        return self.S.op("pe", lambda e: e.matmul(out, lhsT=lhsT, rhs=rhs, start=start, stop=stop),
                         reads=reads, writes=writes)

    def tr(self, out, in_, ident, reads, writes):
        return self.S.op("pe", lambda e: e.transpose(out, in_, ident), reads=reads, writes=writes)

    def act(self, out, in_, func, reads, writes, bias=None, scale=None):
        kw = {}
        if bias is not None:
            kw["bias"] = bias
        if scale is not None:
            kw["scale"] = scale
        return self.S.op("act", lambda e: e.activation(out=out, in_=in_, func=func, **kw), reads=reads, writes=writes)

    def tt(self, eng, out, in0, in1, op, reads, writes):
        return self.S.op(eng, lambda e: e.tensor_tensor(out=out, in0=in0, in1=in1, op=op), reads=reads, writes=writes)

    def ts(self, eng, out, in0, s1, op0, reads, writes, s2=None, op1=None):
        if op1 is None:
            return self.S.op(eng, lambda e: e.tensor_scalar(out=out, in0=in0, scalar1=s1, scalar2=None, op0=op0),
                             reads=reads, writes=writes)
        return self.S.op(eng, lambda e: e.tensor_scalar(out=out, in0=in0, scalar1=s1, scalar2=s2, op0=op0, op1=op1),
                         reads=reads, writes=writes)

    def stt(self, out, in0, scalar, in1, op0, op1, reads, writes):
        return self.S.op("dve", lambda e: e.scalar_tensor_tensor(out=out, in0=in0, scalar=scalar, in1=in1,
                                                                op0=op0, op1=op1), reads=reads, writes=writes)

    def copy(self, eng, out, in_, reads, writes):
        if eng == "act":
            return self.act(out, in_, AF.Identity, reads, writes)
        return self.S.op(eng, lambda e: e.tensor_copy(out=out, in_=in_), reads=reads, writes=writes)

    def memset(self, eng, ap, val, writes):
        return self.S.op(eng, lambda e: e.memset(ap, val), writes=writes)

    def dbg(self, name, ap, shape, keys):
        o = self.dout("dbg_" + name, shape)
        self.outs.append(self.dma("pool", o, ap, reads=keys))

    def dscratch(self, name, shape, dt=F32):
        return self.nc.dram_tensor(name, list(shape), dt, kind="Internal").ap()

    def next_stage(self):
        self.S.barrier()
        self.st.close()
        self.st = contextlib.ExitStack()
        self.stage += 1

    def finish(self):
        self.S.emit(final_wait_ops=self.outs)
        self.st.close()
        self.pst.close()
        return self.nc


def build_post(glu, NT=2048, PT=1024):
    C = Ctx()
    T = {"inT": C.din("inT", [D, NT]), "xres": C.din("xres", [NT, D])}
    if glu:
        T["wglu"] = C.din("wglu", [D, D])
        T["bglu"] = C.din("bglu", [128, 8])
    T["wout"] = C.din("wout", [D, D])
    T["w1"] = C.din("w1", [D, DFF])
    T["b1"] = C.din("b1", [128, 32])
    T["w2"] = C.din("w2", [DFF, D])
    T["vecs"] = C.din("vecs", [5, D])
    T["ident"] = C.din("ident", [128, 128])
    T["xout"] = C.dout("xout", [NT, D])
    C.init_psum()
    emit_post(C, T, glu, NT, PT, final=True)
    return C.finish()


def emit_post(C, T, glu, NT, PT, final, xT_out=None):
    dbg = False
    NTILE = PT // 128
    NCH = PT // 512
    inT, xres = T["inT"], T["xres"]
    if glu:
        wglu, bglu = T["wglu"], T["bglu"]
    wout, w1, b1, w2, vecs, ident, xout = T["wout"], T["w1"], T["b1"], T["w2"], T["vecs"], T["ident"], T["xout"]
    C.ring = 8

    R = C.sb("R", [128, NTILE, D], F32)
    AT0 = C.sb("AT0", [128, 8, PT], BF16)
    AT1 = C.sb("AT1", [128, 8, PT], BF16) if glu else None
    HT = [C.sb(f"HT{i}", [128, 4, PT], BF16) for i in range(2)]
    WO = C.sb("WO", [128, 8, D], BF16)
    W1T = [C.sb(f"W1T{i}", [128, 8, 512], BF16) for i in range(2)]
    W2T = [C.sb(f"W2T{i}", [128, 4, D], BF16) for i in range(2)]
    VB = C.sb("VB", [128, 5, D], F32)
    B1 = C.sb("B1", [128, 32], F32)
    ID = C.sb("ID", [128, 128], F32)
    RELU = [C.sb(f"RELU{i}", [128, 512], F32) for i in range(2)]
    XH = [C.sb(f"XH{i}", [128, D], F32) for i in range(2)]
    ST = [C.sb(f"ST{i}", [128, 2, 6], F32) for i in range(2)]
    MV = [C.sb(f"MV{i}", [128, 2], F32) for i in range(2)]
    RS = [C.sb(f"RS{i}", [128, 1], F32) for i in range(2)]
    NMR = [C.sb(f"NMR{i}", [128, 1], F32) for i in range(2)]
    if glu:
        WG = [C.sb(f"WG{i}", [128, 8, 128], BF16) for i in range(2)]
        GF = [C.sb(f"GF{i}", [128, PT], F32) for i in range(2)]
        SIG = [C.sb(f"SIG{i}", [128, 512], F32) for i in range(2)]
        BG = C.sb("BG", [128, 8], F32)
    XTS = C.sb("XTS", [128, 8, PT], BF16) if xT_out is not None else None

    C.dma("sp", ID[:], ident, writes=["ID"])
    C.dma("sp", B1[:], b1, writes=["B1"])
    if glu:
        C.dma("sp", BG[:], bglu, writes=["BG"])
    for i in range(5):
        C.dma("sp", VB[:, i, :], vecs[i:i + 1, :].partition_broadcast(128), writes=[("VB", i)])
    C.wload(WO[:], wout.rearrange("(k p) n -> p k n", p=128), ["WO"])

    inT_v = inT.rearrange("(k p) t -> p k t", p=128)
    w1_v = w1.rearrange("(k p) n -> p k n", p=128)
    cnt = {"ln": 0}

    def layernorm(t, gi, bi):
        i = cnt["ln"] % 2
        cnt["ln"] += 1
        rk = ("R", t)
        C.S.op("dve", lambda e: e.bn_stats(out=ST[i][:, 0, :], in_=R[:, t, 0:512]), reads=[rk], writes=[("ST", i)])
        C.S.op("dve", lambda e: e.bn_stats(out=ST[i][:, 1, :], in_=R[:, t, 512:1024]), reads=[rk], writes=[("ST", i)])
        C.S.op("dve", lambda e: e.bn_aggr(out=MV[i][:], in_=ST[i][:].rearrange("p a b -> p (a b)")),
               reads=[("ST", i)], writes=[("MV", i)])
        C.act(RS[i][:], MV[i][:, 1:2], AF.Sqrt, [("MV", i)], [("RS", i)], bias=EPS)
        C.S.op("dve", lambda e: e.reciprocal(out=RS[i][:], in_=RS[i][:]), reads=[("RS", i)], writes=[("RS", i)])
        C.stt(NMR[i][:], MV[i][:, 0:1], -1.0, RS[i][:], ALU.mult, ALU.mult, [("MV", i), ("RS", i)], [("NMR", i)])
        C.act(XH[i][:], R[:, t, :], AF.Identity, [rk, ("RS", i), ("NMR", i)], [("XH", i)], bias=NMR[i][:], scale=RS[i][:])
        C.tt("pool", XH[i][:], XH[i][:], VB[:, gi, :], ALU.mult, [("XH", i), ("VB", gi)], [("XH", i)])
        C.tt("pool", R[:, t, :], XH[i][:], VB[:, bi, :], ALU.add, [("XH", i), ("VB", bi)], [rk])

    for p in range(NT // PT):
        tok0 = p * PT
        C.dma("pool", AT0[:], inT_v[:, :, tok0:tok0 + PT], writes=[("AT0", t) for t in range(NTILE)])
        C.dma("sp", R[:], xres[tok0:tok0 + PT, :].rearrange("(t p) d -> p t d", p=128),
              writes=[("R", t) for t in range(NTILE)])
        at0_all = [("AT0", t) for t in range(NTILE)]
        if glu:
            def load_wg(m):
                C.wload(WG[m % 2][:], wglu.rearrange("(k p) n -> p k n", p=128)[:, :, m * 128:(m + 1) * 128],
                        [("WG", m % 2)])
                C.dma("sp", GF[m % 2][:], inT[m * 128:(m + 1) * 128, tok0:tok0 + PT], writes=[("GF", m % 2)])
            load_wg(0)
            for m in range(8):
                if m + 1 < 8:
                    load_wg(m + 1)
                for c in range(NCH):
                    bank, bk = C.psum()
                    for k in range(8):
                        C.mm(bank[:, :], WG[m % 2][:, k, :], AT0[:, k, c * 512:(c + 1) * 512], k == 0, k == 7,
                             [("WG", m % 2)] + at0_all, [bk])
                    si = (m * NCH + c) % 2
                    C.act(SIG[si][:], bank[:, :], AF.Sigmoid, [bk, "BG"], [("SIG", si)], bias=BG[:, m:m + 1])
                    C.tt("dve", AT1[:, m, c * 512:(c + 1) * 512], GF[m % 2][:, c * 512:(c + 1) * 512], SIG[si][:],
                         ALU.mult, [("GF", m % 2), ("SIG", si)], [("AT1", m, c)])
            src = AT1
            if dbg:
                C.outs.append(C.dma("pool", dbg_glu.rearrange("(k p) t -> p k t", p=128)[:, :, tok0:tok0 + PT], AT1[:], reads=[("AT1", m, c) for m in range(8) for c in range(NCH)]))
            src_keys = lambda t: [("AT1", m, t // 4) for m in range(8)]
        else:
            src = AT0
            src_keys = lambda t: [("AT0", t)]

        def outproj(t):
            for n in range(2):
                bank, bk = C.psum()
                for k in range(8):
                    C.mm(bank[:, :], src[:, k, t * 128:(t + 1) * 128], WO[:, k, n * 512:(n + 1) * 512], k == 0, k == 7,
                         ["WO"] + src_keys(t), [bk])
                C.stt(R[:, t, n * 512:(n + 1) * 512], R[:, t, n * 512:(n + 1) * 512], ALPHA, bank[:, :],
                      ALU.mult, ALU.add, [("R", t), bk], [("R", t)])
            layernorm(t, 0, 1)

        def transposes(t):
            for h in range(2):
                bank, bk = C.psum()
                for kk in range(4):
                    k = 4 * h + kk
                    C.tr(bank[:, kk * 128:(kk + 1) * 128], R[:, t, k * 128:(k + 1) * 128], ID[:], [("R", t), "ID"], [bk])
                C.act(AT0[:, 4 * h:4 * h + 4, t * 128:(t + 1) * 128], bank[:, :].rearrange("p (k c) -> p k c", k=4),
                      AF.Identity, [bk], [("AT0", t)])
            if dbg:
                C.outs.append(C.dma("sp", dbg_xa[tok0 + t * 128: tok0 + (t + 1) * 128, :], R[:, t, :], reads=[("R", t)]))
            C.stt(R[:, t, :], R[:, t, :], ALPHA, VB[:, 2, :], ALU.mult, ALU.add, [("R", t), ("VB", 2)], [("R", t)])

        for t in range(NTILE + 1):
            if t < NTILE:
                outproj(t)
            if t >= 1:
                transposes(t - 1)

        NJB = DFF // 512

        def load_w1(jb):
            C.wload(W1T[jb % 2][:], w1_v[:, :, jb * 512:(jb + 1) * 512], [("W1T", jb % 2)], alt=0)

        def load_w2(jb):
            C.wload(W2T[jb % 2][:], w2[jb * 512:(jb + 1) * 512, :].rearrange("(j p) n -> p j n", p=128),
                    [("W2T", jb % 2)], alt=1)

        def mlp1(jb):
            b = jb % 2
            for jj in range(4):
                j = jb * 4 + jj
                for c in range(NCH):
                    bank, bk = C.psum()
                    for k in range(8):
                        C.mm(bank[:, :], W1T[b][:, k, jj * 128:(jj + 1) * 128], AT0[:, k, c * 512:(c + 1) * 512],
                             k == 0, k == 7, [("W1T", b)] + at0_all, [bk])
                    ri = (jj * NCH + c) % 2
                    C.act(RELU[ri][:], bank[:, :], AF.Relu, [bk, "B1"], [("RELU", ri)], bias=B1[:, j:j + 1])
                    C.stt(HT[b][:, jj, c * 512:(c + 1) * 512], bank[:, :], B1[:, j:j + 1], RELU[ri][:], ALU.add, ALU.mult,
                          [bk, "B1", ("RELU", ri)], [("HT", b, c)])

        def mlp2(jb):
            b = jb % 2
            for t in range(NTILE):
                for n in range(2):
                    bank, bk = C.psum()
                    for jj in range(4):
                        C.mm(bank[:, :], HT[b][:, jj, t * 128:(t + 1) * 128], W2T[b][:, jj, n * 512:(n + 1) * 512],
                             jj == 0, jj == 3, [("W2T", b), ("HT", b, t // 4)], [bk])
                    C.tt("dve", R[:, t, n * 512:(n + 1) * 512], bank[:, :], R[:, t, n * 512:(n + 1) * 512], ALU.add,
                         [bk, ("R", t)], [("R", t)])

        load_w1(0)
        load_w2(0)
        for jb in range(NJB + 1):
            if jb + 1 < NJB:
                load_w1(jb + 1)
            if jb < NJB:
                mlp1(jb)
            if jb >= 1:
                mlp2(jb - 1)
            if jb + 1 < NJB:
                load_w2(jb + 1)

        for t in range(NTILE):
            layernorm(t, 3, 4)
            o = C.dma("sp", xout[tok0 + t * 128: tok0 + (t + 1) * 128, :], R[:, t, :], reads=[("R", t)],
                      writes=[("xout", (tok0 + t * 128) // 512)])
            if final:
                C.outs.append(o)
            if xT_out is not None:
                for h in range(2):
                    bank, bk = C.psum()
                    for kk in range(4):
                        k = 4 * h + kk
                        C.tr(bank[:, kk * 128:(kk + 1) * 128], R[:, t, k * 128:(k + 1) * 128], ID[:], [("R", t), "ID"], [bk])
                    C.act(XTS[:, 4 * h:4 * h + 4, t * 128:(t + 1) * 128], bank[:, :].rearrange("p (k c) -> p k c", k=4),
                          AF.Identity, [bk], [("XTS", t)])
        if xT_out is not None:
            C.dma("sp", xT_out.rearrange("(k p) t -> p k t", p=128)[:, :, tok0:tok0 + PT], XTS[:],
                  reads=[("XTS", t) for t in range(NTILE)], writes=[("xT_out", tok0 // 1024)])


def build_attn():
    C = Ctx()
    T = {"x1T": C.din("x1T", [D, SEQ]), "wq": C.din("wq", [D, 512]), "wk": C.din("wk", [D, 512]),
         "wv": C.din("wv", [D, 512]), "ident": C.din("ident", [128, 128]), "masku": C.din("masku", [128, 128]),
         "oT": C.dout("oT", [512, SEQ])}
    C.init_psum()
    emit_attn(C, T, 4, final=True)
    return C.finish()


ATT_NB = 5
ATT_NPT = 10
ATT_LAGS = (0, 2, 3, 5, 7)


def emit_attn(C, T, NPAIR, final, x_bf16=False, half=False):
    C.ring = 6
    wq, wk, wv, ident, masku, oT = T["wq"], T["wk"], T["wv"], T["ident"], T["masku"], T["oT"]
    x1T = T.get("x1T")
    QLEN = 2048 if half else SEQ

    XT = C.sb("XT", [128, 8, SEQ], BF16)
    QT = [C.sb(f"QT{i}", [128, QLEN], BF16) for i in range(2)]
    KT = [C.sb(f"KT{i}", [128, SEQ], BF16) for i in range(2)]
    V = [C.sb(f"V{i}", [128, 32, 128], BF16) for i in range(2)]
    WQ = [C.sb(f"WQ{i}", [128, 8, 128], BF16) for i in range(2)]
    WK = [C.sb(f"WK{i}", [128, 8, 128], BF16) for i in range(2)]
    WV = [C.sb(f"WV{i}", [128, 8, 128], BF16) for i in range(2)]
    NB = ATT_NB
    NP8 = ATT_NPT
    PT = [C.sb(f"PT{i}", [128, 516], F32) for i in range(NP8)]
    OMB = [C.sb(f"OMB{i}", [128, 516], F32) for i in range(NB)]
    WW = [C.sb(f"WW{i}", [128, 512], BF16) for i in range(NB)]
    WT = [C.sb(f"WT{i}", [128, 512], BF16) for i in range(NB)]
    OS = [C.sb(f"OS{i}", [128, 128], F32) for i in range(4)]
    ONE = C.sb("ONE", [128, 1], F32)
    ZERO = C.sb("ZERO", [128, 516], F32)
    ID = C.sb("ID", [128, 128], F32)
    IDB = C.sb("IDB", [128, 128], BF16)
    MASKU = C.sb("MASKU", [128, 128], F32)

    C.dma("sp", ID[:], ident, writes=["ID"])
    C.dma("sp", MASKU[:], masku, writes=["MASKU"])
    C.copy("dve", IDB[:], ID[:], ["ID"], ["IDB"])
    C.memset("pool", ZERO[:], 0.0, ["ZERO"])
    C.memset("pool", ONE[:], 1.0, ["ONE"])
    for j_ in range(NB):
        C.memset("pool", OMB[j_][:, 512:513], 1.0, [("OMB", j_)])
    C.memset("pool", ZERO[:], 0.0, ["ZERO"])
    if not half:
        x_v = x1T.rearrange("(k p) t -> p k t", p=128)
        for c in range(4):
            C.dma("sp" if x_bf16 else "pool", XT[:, :, c * 1024:(c + 1) * 1024], x_v[:, :, c * 1024:(c + 1) * 1024],
                  reads=[("xT_out", c)], writes=[("XT", c)])
    else:
        KI = C.sb("KI", [128, 32], U32)
        QI = C.sb("QI", [128, 16], U32)
        M2 = C.sb("M2", [128, 256], F32)
        XQ = C.sb("XQ", [128, 8, QLEN], BF16)
        NG = 3
        G = [C.sb(f"G{i}", [128, D], F32) for i in range(NG)]
        C.dma("sp", KI[:], T["kidx"], writes=["KI"])
        C.dma("sp", QI[:], T["qidx"], writes=["QI"])
        C.dma("sp", M2[:], T["mask2"], writes=["M2"])
        gcnt = {"n": 0}

        def gather(idx_ap, dst, j, key, own):
            gi = gcnt["n"] % NG
            gcnt["n"] += 1
            g = G[gi]
            if own:
                C.S.op("pool", lambda e: e.indirect_dma_start(out=g[:], out_offset=None, in_=T["x1"],
                                                              in_offset=bass.IndirectOffsetOnAxis(ap=idx_ap, axis=0)),
                       reads=["KI", "QI"], writes=[("G", gi)], dma=True)
            else:
                C.dma("sp", g[:], T["x1"][j * 128:(j + 1) * 128, :], writes=[("G", gi)])
            if own:
                C.dma("sp", T["x1own"][j * 128:(j + 1) * 128, :], g[:], reads=[("G", gi)], writes=[("x1own", j)])
            for h in range(2):
                bank, bk = C.psum()
                for kk in range(4):
                    k = 4 * h + kk
                    C.tr(bank[:, kk * 128:(kk + 1) * 128], g[:, k * 128:(k + 1) * 128], ID[:], [("G", gi), "ID"], [bk])
                C.act(dst[:, 4 * h:4 * h + 4, j * 128:(j + 1) * 128], bank[:, :].rearrange("p (k c) -> p k c", k=4),
                      AF.Identity, [bk], [key])

        for j in range(32):
            gather(KI[:, j:j + 1], XT, j, ("XT", j // 8), False)
        for j in range(16):
            gather(QI[:, j:j + 1], XQ, j, ("XQ", j // 4), True)

    def load_w(pc):
        b = pc % 2
        for W, src, nm in ((WQ, wq, "WQ"), (WK, wk, "WK"), (WV, wv, "WV")):
            C.wload(W[b][:], src.rearrange("(k p) n -> p k n", p=128)[:, :, pc * 128:(pc + 1) * 128], [(nm, b)])

    def project(pc):
        b = pc % 2
        for c in range(8):
            for W, dst, nm, dnm, sc in ((WQ, QT, "WQ", "QT", 0.125), (WK, KT, "WK", "KT", None)):
                if sc is not None and c * 512 >= QLEN:
                    continue
                src, skey = (XQ, ("XQ", c)) if (half and sc is not None) else (XT, ("XT", c // 2))
                bank, bk = C.psum()
                for k in range(8):
                    C.mm(bank[:, :], W[b][:, k, :], src[:, k, c * 512:(c + 1) * 512], k == 0, k == 7,
                         [(nm, b), skey], [bk])
                if sc is not None:
                    C.act(dst[b][:, c * 512:(c + 1) * 512], bank[:, :], AF.Identity, [bk], [(dnm, b, c)], scale=sc)
                else:
                    C.copy("dve", dst[b][:, c * 512:(c + 1) * 512], bank[:, :], [bk], [(dnm, b, c)])
        for t4 in range(8):
            bank, bk = C.psum()
            for tt_ in range(4):
                t = t4 * 4 + tt_
                for k in range(8):
                    C.mm(bank[:, tt_ * 128:(tt_ + 1) * 128], XT[:, k, t * 128:(t + 1) * 128], WV[b][:, k, :],
                         k == 0, k == 7, [("WV", b), ("XT", t // 8)], [bk])
            C.copy("act" if t4 % 2 else "dve", V[b][:, t4 * 4:(t4 + 1) * 4, :],
                   bank[:, :].rearrange("p (t c) -> p t c", t=4), [bk], [("V", b, t4)])

    def steps_for(pc):
        st = []
        for i in range(QLEN // 128):
            top = gblk(i) // 4
            for kt in range(top, -1, -1):
                for hl in range(2):
                    st.append((pc, hl, i, kt, top))
        return st

    def gblk(i):
        return 2 * i + 1 if half else i

    cnt = {"q": 0}
    obank = {}

    def width(s):
        pc, hl, i, kt, top = s
        return ((gblk(i) % 4) + 1) * 128 if kt == top else 512

    def stA(n, s):
        pc, hl, i, kt, top = s
        b = pc % 2
        j = n % NB
        W = width(s)
        s0 = kt * 512
        base = 64 * hl
        if kt == top:
            obank[(pc, hl, i)] = 6 + hl
        zb, zk = C.psum()
        C.mm(zb[:, :W], QT[b][base:base + 64, i * 128:(i + 1) * 128], KT[b][base:base + 64, s0:s0 + W], True, True,
             [("QT", b, i // 4), ("KT", b, kt)], [zk])
        C.act(OMB[j][:, :W], zb[:, :W], AF.Sigmoid, [zk], [("OMB", j)], scale=-1.0)
        if kt == top and half:
            C.tt("dve", OMB[j][:, W - 256:W], OMB[j][:, W - 256:W], M2[:], ALU.max, [("OMB", j), "M2"], [("OMB", j)])
        elif kt == top:
            C.tt("dve", OMB[j][:, W - 128:W], OMB[j][:, W - 128:W], MASKU[:], ALU.max, [("OMB", j), "MASKU"], [("OMB", j)])

    def stB(n, s):
        pc, hl, i, kt, top = s
        j = n % NB
        p8 = n % NP8
        W = width(s)
        if kt == top:
            C.memset("pool", PT[p8][:, W:W + 1], 1.0, [("PT", p8)])
            C.S.op("dve", lambda e: e.tensor_tensor_scan(out=PT[p8][:, 0:W][:, ::-1], data0=OMB[j][:, 0:W][:, ::-1],
                                                        data1=ZERO[:, 0:W], initial=1.0, op0=ALU.mult, op1=ALU.add),
                   reads=[("OMB", j), "ZERO"], writes=[("PT", p8)])
        else:
            carry, ck = PT[(n - 2) % NP8][:, 0:1], ("PT", (n - 2) % NP8)
            C.S.op("dve", lambda e: e.tensor_tensor_scan(out=PT[p8][:, 0:513][:, ::-1], data0=OMB[j][:, 0:513][:, ::-1],
                                                        data1=ZERO[:, 0:513], initial=carry, op0=ALU.mult, op1=ALU.add),
                   reads=[("OMB", j), "ZERO", ck], writes=[("PT", p8)])

    def stB2(n, s):
        j = n % NB
        p8 = n % NP8
        W = width(s)
        C.tt("pool", WW[j][:, 0:W], PT[p8][:, 1:W + 1], PT[p8][:, 0:W], ALU.subtract, [("PT", p8)], [("WW", j)])

    def stC(n, s):
        j = n % NB
        W = width(s)
        tb, tk = C.psum()
        for blk in range(W // 128):
            C.mm(tb[:, blk * 128:(blk + 1) * 128], WW[j][:, blk * 128:(blk + 1) * 128], IDB[:], True, True,
                 [("WW", j), "IDB"], [tk])
        C.copy("act", WT[j][:, :W], tb[:, :W], [tk], [("WT", j)])

    def stD(n, s):
        pc, hl, i, kt, top = s
        b = pc % 2
        j = n % NB
        W = width(s)
        ob = C.banks[obank[(pc, hl, i)]]
        obk = ("ps", obank[(pc, hl, i)])
        nb = W // 128
        for blk in range(nb):
            stile = kt * 4 + blk
            C.mm(ob[0:64, 0:128], V[b][:, stile, hl * 64:(hl + 1) * 64], WT[j][:, blk * 128:(blk + 1) * 128],
                 kt == top and blk == 0, kt == 0 and blk == nb - 1, [("V", b, stile // 4), ("WT", j)], [obk])
        if kt == 0:
            oi = cnt["q"] % 4
            cnt["q"] += 1
            C.copy("act", OS[oi][0:64, :], ob[0:64, 0:128], [obk], [("OS", oi)])
            h = 2 * pc + hl
            o = C.dma("sp", oT[h * 64:(h + 1) * 64, i * 128:(i + 1) * 128], OS[oi][0:64, :], reads=[("OS", oi)],
                      writes=[("oT_d", h, i)])
            if final:
                C.outs.append(o)

    stages = tuple(zip((stA, stB, stB2, stC, stD), ATT_LAGS))
    load_w(0)
    project(0)
    for pc in range(NPAIR):
        if pc + 1 < NPAIR:
            load_w(pc + 1)
        steps = steps_for(pc)
        N = len(steps)
        maxlag = max(l for _, l in stages)
        for it in range(N + maxlag):
            for fn, lag in stages:
                idx = it - lag
                if 0 <= idx < N:
                    fn(idx, steps[idx])
            if it == N // 2 and pc + 1 < NPAIR:
                project(pc + 1)


GH_ORDER = (0, 1)


def build_s5():
    C = Ctx()
    T = {"xT": C.din("xT", [D, SEQ]), "win": C.din("win", [D, 512]), "lamre": C.din("lamre", [128, 32]),
         "lamim": C.din("lamim", [128, 32]), "logstep": C.din("logstep", [128, 32]),
         "bst": C.din("bst", [128, 32, 16]), "bsw": C.din("bsw", [128, 32, 16]),
         "cst": C.din("cst", [128, 32, 16]), "csw": C.din("csw", [128, 32, 16]),
         "dvec": C.din("dvec", [1, 512]), "ident": C.din("ident", [128, 128]),
         "cmask": C.din("cmask", [128, 2, 256]), "sign": C.din("sign", [128, 1]),
         "g_out": C.dout("g_out", [SEQ, 512])}
    C.init_psum()
    emit_s5(C, T, final=True)
    return C.finish()


def emit_s5(C, T, final, gT_out=None, after_uproj=None):
    dbg = False
    C.ring = 8
    xT, win, lamre, lamim, logstep = T["xT"], T["win"], T["lamre"], T["lamim"], T["logstep"]
    bst, bsw, cst, csw, dvec = T["bst"], T["bsw"], T["cst"], T["csw"], T["dvec"]
    ident, cmask, sign = T["ident"], T["cmask"], T["sign"]
    g_out = T.get("g_out")

    XTZ = C.sb("XTZ", [128, 8192], F32)
    XT = XTZ[:, :].bitcast(BF16).rearrange("p (k t) -> p k t", k=8)
    Z = XTZ[:, :].rearrange("p (s g c) -> p s g c", s=2, g=16)
    WINB = C.sb("WINB", [128, 2048], F32)
    WIN = WINB[:, :].bitcast(BF16).rearrange("p (k n) -> p k n", k=8)
    U = C.sb("U", [128, 2, 32, 16, 16], BF16)
    ID = C.sb("ID", [128, 128], F32)
    IDB = C.sb("IDB", [128, 128], BF16)
    CM = C.sb("CM", [128, 2, 256], F32)
    SG = C.sb("SGN", [128, 1], F32)
    NSG = C.sb("NSG", [128, 1], F32)
    DBC = C.sb("DBC", [128, 512], F32)
    BST = C.sb("BST", [128, 16, 16], F32)
    BSW = C.sb("BSW", [128, 16, 16], F32)
    CST = C.sb("CST", [128, 16, 16], F32)
    CSW = C.sb("CSW", [128, 16, 16], F32)
    GTS = WINB[:, :].rearrange("p (c r) -> p c r", r=16) if gT_out is not None else None

    def pt(name):
        return (name, C.sb(name, [128, 32], F32))

    def v2(out, a, b, op, eng="dve"):
        C.tt(eng, out[1][:], a[1][:], b[1][:], op, [a[0], b[0]], [out[0]])

    def vs(out, a, s1, op0, s2=None, op1=None, eng="dve"):
        C.ts(eng, out[1][:], a[1][:], s1, op0, [a[0]], [out[0]], s2=s2, op1=op1)

    def vact(out, a, func, scale=None, bias=None):
        C.act(out[1][:], a[1][:], func, [a[0]], [out[0]], bias=bias, scale=scale)

    C.dma("sp", ID[:], ident, writes=["ID"])
    C.copy("dve", IDB[:], ID[:], ["ID"], ["IDB"])
    C.dma("sp", CM[:], cmask, writes=["CM"])
    C.dma("sp", SG[:], sign, writes=["SGN"])
    C.ts("dve", NSG[:], SG[:], -1.0, ALU.mult, ["SGN"], ["NSG"])
    C.dma("sp", DBC[:], dvec.partition_broadcast(128), writes=["DBC"])
    LR, LI, DT = pt("LR"), pt("LI"), pt("DT")
    C.dma("sp", LR[1][:], lamre, writes=["LR"])
    C.dma("sp", LI[1][:], lamim, writes=["LI"])
    C.dma("sp", DT[1][:], logstep, writes=["DT"])
    C.dma("pool", WIN, win.rearrange("(k p) n -> p k n", p=128), writes=["WIN"])

    vact(DT, DT, AF.Exp)
    LD, MAG, M2, ANG = pt("LD"), pt("MAG"), pt("M2"), pt("ANG")
    v2(LD, LR, DT, ALU.mult)
    vact(MAG, LD, AF.Exp)
    vact(M2, LD, AF.Exp, scale=-2.0)
    v2(ANG, LI, DT, ALU.mult)
    CC, SS, T1, T2, T3 = pt("CC"), pt("SS"), pt("T1"), pt("T2"), pt("T3")
    HP = C.sb("HP", [128, 1], F32)
    C.memset("dve", HP[:], math.pi / 2, ["HP"])
    C.act(CC[1][:], ANG[1][:], AF.Sin, ["ANG", "HP"], ["CC"], bias=HP[:], scale=1.0 / 16)
    vact(SS, ANG, AF.Sin, scale=1.0 / 16)

    def csquare(X, Y):
        v2(T1, X, X, ALU.mult)
        v2(T2, Y, Y, ALU.mult)
        v2(T3, X, Y, ALU.mult)
        v2(X, T1, T2, ALU.subtract)
        vs(Y, T3, 2.0, ALU.mult)

    for _ in range(4):
        csquare(CC, SS)
    AR, AI = pt("AR"), pt("AI")
    v2(AR, MAG, CC, ALU.mult)
    v2(AI, MAG, SS, ALU.mult)
    NR, DEN, FRE, FIM = pt("NR"), pt("DEN"), pt("FRE"), pt("FIM")
    vs(NR, AR, -1.0, ALU.add)
    v2(T1, LR, LR, ALU.mult)
    v2(T2, LI, LI, ALU.mult)
    v2(DEN, T1, T2, ALU.add)
    C.S.op("dve", lambda e: e.reciprocal(out=DEN[1][:], in_=DEN[1][:]), reads=["DEN"], writes=["DEN"])
    v2(T1, NR, LR, ALU.mult)
    v2(T2, AI, LI, ALU.mult)
    v2(T1, T1, T2, ALU.add)
    v2(FRE, T1, DEN, ALU.mult)
    v2(T1, AI, LR, ALU.mult)
    v2(T2, NR, LI, ALU.mult)
    v2(T1, T1, T2, ALU.subtract)
    v2(FIM, T1, DEN, ALU.mult)
    AIS, FIS, AVR, AVIS = pt("AIS"), pt("FIS"), pt("AVR"), pt("AVIS")
    vs(AIS, AI, SG[:], ALU.mult)
    vs(FIS, FIM, SG[:], ALU.mult)
    v2(AVR, AR, M2, ALU.mult)
    v2(AVIS, AI, M2, ALU.mult)
    vs(AVIS, AVIS, NSG[:], ALU.mult)
    XR, XI = pt("XR"), pt("XI")
    C.copy("dve", XR[1][:], AR[1][:], ["AR"], ["XR"])
    C.copy("dve", XI[1][:], AI[1][:], ["AI"], ["XI"])
    for _ in range(4):
        csquare(XR, XI)
    A16IS = pt("A16IS")
    vs(A16IS, XI, SG[:], ALU.mult)
    A16R = XR
    YR, YI = pt("YR"), pt("YI")
    C.copy("dve", YR[1][:], XR[1][:], ["XR"], ["YR"])
    C.copy("dve", YI[1][:], XI[1][:], ["XI"], ["YI"])
    for _ in range(4):
        csquare(YR, YI)
    A256IS = pt("A256IS")
    vs(A256IS, YI, SG[:], ALU.mult)
    A256R = YR

    xT_v = xT.rearrange("(k p) t -> p k t", p=128)
    ev = 0
    for ct in range(2):
        C.dma("pool", XT, xT_v[:, :, ct * 2048:(ct + 1) * 2048], writes=["XT", "Zr"] + [("Z", g) for g in range(16)])
        for r in range(16):
            bank, bk = C.psum()
            for k in range(8):
                C.mm(bank[:, :], XT[:, k, r::16], WIN[:, k, :], k == 0, k == 7, ["XT", "WIN"], [bk])
            C.copy("act", U[:, ct, :, r, :], bank[:, :].rearrange("p (g h) -> p g h", g=32), [bk], [("U", ct)])
            ev += 1

    if after_uproj is not None:
        after_uproj()
    def tb(name, shape, dt=F32):
        return (name, C.sb(name, shape, dt))

    ARb, AISb, AVRb, AVISb, FRb, FISb = [tb(n, [128, 16, 16]) for n in ("ARb", "AISb", "AVRb", "AVISb", "FRb", "FISb")]
    SCR = C.sb("SCR", [128, 2, 2, 256], F32)
    CS = [("SCRa", SCR[:, 0, i, :].rearrange("p (g h) -> p g h", g=16)) for i in range(2)]
    CW = [("SCRb", SCR[:, 1, i, :].rearrange("p (g h) -> p g h", g=16)) for i in range(2)]
    BBS, BBW = tb("BBS", [128, 16, 16]), tb("BBW", [128, 16, 16])
    TA, TB = tb("TA", [128, 16, 16]), tb("TB", [128, 16, 16])
    PL = tb("PL", [128, 16, 16, 16], BF16)
    QR = tb("QR", [128, 16, 16, 16], BF16)
    INS = tb("INS", [128, 16, 16, 16], BF16)
    CA = tb("CA", [128, 16, 16, 16], BF16)
    IN = tb("IN", [128, 16, 2, 128], BF16)
    INW = tb("INW", [128, 16, 2, 128], BF16)
    TOEP = tb("TOEP", [128, 16, 2, 256], BF16)
    UT = tb("UT", [128, 16, 2, 256], BF16)
    SPB = tb("SPB", [128, 16, 256], BF16)
    RR, II = tb("RR", [128, 2, 16]), tb("II", [128, 2, 16])
    ZT1, ZT2 = tb("ZT1", [128, 2, 16]), tb("ZT2", [128, 2, 16])
    RR2, II2 = tb("RR2", [128, 2, 16]), tb("II2", [128, 2, 16])
    BT1 = ("SCRa", SCR[:, 0, :, :].rearrange("p a (g b) -> p a g b", g=16))
    BT2 = ("SCRb", SCR[:, 1, :, :].rearrange("p a (g b) -> p a g b", g=16))
    CCY = tb("CCY", [128, 2, 16, 16])
    YQ = [tb("YQ_0", [128, 16, 128])] * 2
    X2 = [tb("X2_0", [128, 16, 128])] * 2

    def bc(out, src, gh):
        C.copy("dve", out[1][:], src[1][:, gh * 16:(gh + 1) * 16].unsqueeze(2).broadcast_to([128, 16, 16]),
               [src[0]], [out[0]])

    def cmul(outS, outW, inS, inW, Rb, Isb):
        C.tt("dve", TA[1][:], Rb[1][:], inS[1], ALU.mult, [Rb[0], inS[0]], ["TA"])
        C.tt("dve", TB[1][:], Isb[1][:], inW[1], ALU.mult, [Isb[0], inW[0]], ["TB"])
        C.tt("dve", outS[1], TA[1][:], TB[1][:], ALU.add, ["TA", "TB"], [outS[0]])
        C.tt("dve", TA[1][:], Rb[1][:], inW[1], ALU.mult, [Rb[0], inW[0]], ["TA"])
        C.tt("dve", TB[1][:], Isb[1][:], inS[1], ALU.mult, [Isb[0], inS[0]], ["TB"])
        C.tt("dve", outW[1], TA[1][:], TB[1][:], ALU.subtract, ["TA", "TB"], [outW[0]])

    ycnt = {"n": 0}
    for gh in GH_ORDER:
        gsl = slice(gh * 16, (gh + 1) * 16)
        for o_, s_ in ((ARb, AR), (AISb, AIS), (AVRb, AVR), (AVISb, AVIS), (FRb, FRE), (FISb, FIS)):
            bc(o_, s_, gh)
        for tl, src, nm in ((BST, bst, "BST"), (BSW, bsw, "BSW"), (CST, cst, "CST"), (CSW, csw, "CSW")):
            C.dma("sp", tl[:], src[:, gsl, :], writes=[nm])
        cmul(("BBS", BBS[1][:]), ("BBW", BBW[1][:]), ("BST", BST[:]), ("BSW", BSW[:]), FRb, FISb)

        if dbg and gh == 1:
            C.dbg("ARb", ARb[1][:], [128, 16, 16], ["ARb"])
            C.dbg("AVISb", AVISb[1][:], [128, 16, 16], ["AVISb"])
            C.dbg("BST", BST[:], [128, 16, 16], ["BST"])
            C.dbg("BBS", BBS[1][:], [128, 16, 16], ["BBS"])
            C.dbg("AR", AR[1][:], [128, 32], ["AR"])
            C.dbg("FRE", FRE[1][:], [128, 32], ["FRE"])

        def run_steps(initS, initW, Rb, Isb, nsteps, emit):
            cur = (initS, initW)
            for k in range(nsteps):
                emit(k, cur[0])
                if k + 1 < nsteps:
                    nxt = (CS[k % 2], CW[k % 2])
                    cmul(nxt[0], nxt[1], cur[0], cur[1], Rb, Isb)
                    cur = nxt

        def emit_P(k, curS):
            C.ts("dve", PL[1][:, :, k, :], curS[1], NSG[:], ALU.mult, [curS[0], "NSG"], ["PL"])
        run_steps(("BBS", BBS[1][:]), ("BBW", BBW[1][:]), AVRb, AVISb, 16, emit_P)

        def emit_IN(k, curS):
            C.copy("dve", INS[1][:, :, 15 - k, :], curS[1], [curS[0]], ["INS"])
        run_steps(("BBS", BBS[1][:]), ("BBW", BBW[1][:]), ARb, AISb, 16, emit_IN)

        def emit_Q(k, curS):
            if k <= 15:
                C.copy("dve", QR[1][:, :, k, :], curS[1], [curS[0]], ["QR"])
            if k >= 1:
                C.ts("dve", CA[1][:, :, k - 1, :], curS[1], NSG[:], ALU.mult, [curS[0], "NSG"], ["CA"])
        run_steps(("CST", CST[:]), ("CSW", CSW[:]), ARb, AISb, 17, emit_Q)

        for g in range(16):
            bank, bk = C.psum()
            for kt in range(2):
                C.mm(bank[:, kt * 256:(kt + 1) * 256], PL[1][:, g, kt * 8:(kt + 1) * 8, :], QR[1][:, g, :, :], True, True,
                     ["PL", "QR"], [bk])
            C.tt("dve", TOEP[1][:, g, :, :], bank[:, :].rearrange("p (k n) -> p k n", k=2), CM[:], ALU.mult,
                 [bk, "CM"], [("TOEP", g)])
        if dbg and gh == 1:
            C.dbg("PL", PL[1][:], [128, 16, 16, 16], ["PL"])
            C.dbg("QR", QR[1][:], [128, 16, 16, 16], ["QR"])
            C.dbg("CA", CA[1][:], [128, 16, 16, 16], ["CA"])
            C.dbg("TOEP", TOEP[1][:], [128, 16, 2, 256], [("TOEP", g) for g in range(16)])
        for g2 in range(8):
            bank, bk = C.psum()
            tbf = bank[:, :].bitcast(BF16)
            for gi in range(2):
                g = g2 * 2 + gi
                for kt in range(2):
                    col = (gi * 2 + kt) * 128
                    C.tr(tbf[:, col:col + 128], INS[1][:, g, kt * 8:(kt + 1) * 8, :], IDB[:], ["INS", "IDB"], [bk])
            C.copy("act", IN[1][:, g2 * 2:g2 * 2 + 2, :, :], tbf[:, 0:512].rearrange("p (g k s) -> p g k s", g=2, k=2),
                   [bk], [("IN", g2)])
            for hf in range(2):
                C.copy("dve", INW[1][:, g2 * 2:g2 * 2 + 2, :, hf * 64:(hf + 1) * 64],
                       IN[1][:, g2 * 2:g2 * 2 + 2, :, (1 - hf) * 64:(2 - hf) * 64], [("IN", g2)], [("INW", g2)])
        for g in range(16):
            G = gh * 16 + g
            bank, bk = C.psum()
            tbf = bank[:, :].bitcast(BF16)
            for kt in range(2):
                for ct in range(2):
                    col = kt * 256 + ct * 128
                    C.tr(tbf[:, col:col + 128], U[:, ct, G, kt * 8:(kt + 1) * 8, :], IDB[:],
                         [("U", ct), "IDB"], [bk])
            C.copy("act" if g % 2 else "dve", UT[1][:, g, :, :], tbf[:, 0:512].rearrange("p (k c) -> p k c", k=2),
                   [bk], [("UT", g)])
        for g in range(16):
            bank, bk = C.psum()
            for sw in range(2):
                for kt in range(2):
                    lhsT = (INW if sw else IN)[1][:, g, kt, :]
                    C.mm(bank[:, sw * 256:(sw + 1) * 256], lhsT, UT[1][:, g, kt, :], kt == 0, kt == 1,
                         [("IN", g // 2), ("INW", g // 2), ("UT", g)], [bk])
            C.copy("act" if g % 2 else "dve", Z[:, :, g, :], bank[:, :].rearrange("p (s c) -> p s c", s=2),
                   [bk, "XT", "Zdone"], [("Z", g)])
        zall = [("Z", g) for g in range(16)]
        for s in range(2):
            C.copy("dve", RR[1][:, s, :], A16R[1][:, gsl], [A16R[0]], ["RR"])
        C.copy("dve", II[1][:, 0, :], A16IS[1][:, gsl], ["A16IS"], ["II"])
        C.ts("dve", II[1][:, 1, :], A16IS[1][:, gsl], -1.0, ALU.mult, ["A16IS"], ["II"])
        for s_ in range(2):
            C.copy("dve", RR2[1][:, s_, :], A256R[1][:, gsl], [A256R[0]], ["RR2"])
        C.copy("dve", II2[1][:, 0, :], A256IS[1][:, gsl], ["A256IS"], ["II2"])
        C.ts("dve", II2[1][:, 1, :], A256IS[1][:, gsl], -1.0, ALU.mult, ["A256IS"], ["II2"])
        Z5 = Z.rearrange("p s g (b j) -> p s g b j", j=16)
        RRb = RR[1][:].unsqueeze(3).broadcast_to([128, 2, 16, 16])
        IIb = II[1][:].unsqueeze(3).broadcast_to([128, 2, 16, 16])
        for j in range(1, 16):
            X = Z5[:, :, :, :, j - 1]
            C.tt("dve", BT1[1][:], RRb, X, ALU.mult, ["RR"] + (zall if j == 1 else ["Zr"]), ["SCRa"])
            C.tt("dve", BT2[1][:], IIb, X[:, ::-1], ALU.mult, ["II", "Zr"] + (zall if j == 1 else []), ["SCRb"])
            C.tt("dve", Z5[:, :, :, :, j], Z5[:, :, :, :, j], BT1[1][:], ALU.add, ["SCRa", "Zr"], ["Zr"])
            C.tt("dve", Z5[:, :, :, :, j], Z5[:, :, :, :, j], BT2[1][:], ALU.add, ["SCRb", "Zr"], ["Zr"])
        for b_ in range(1, 16):
            X = Z5[:, :, :, b_ - 1, 15]
            C.tt("dve", ZT1[1][:], RR2[1][:], X, ALU.mult, ["RR2", "Zr"], ["ZT1"])
            C.tt("dve", ZT2[1][:], II2[1][:], X[:, ::-1], ALU.mult, ["II2", "Zr"], ["ZT2"])
            C.tt("dve", Z5[:, :, :, b_, 15], Z5[:, :, :, b_, 15], ZT1[1][:], ALU.add, ["ZT1", "Zr"], ["Zr"])
            C.tt("dve", Z5[:, :, :, b_, 15], Z5[:, :, :, b_, 15], ZT2[1][:], ALU.add, ["ZT2", "Zr"], ["Zr"])
        CC = CCY[1][:, :, :, 0:15]
        C.copy("dve", CC, Z5[:, :, :, 0:15, 15], ["Zr"], ["CCY"])
        for j in range(15):
            C.tt("dve", BT1[1][:, :, :, 0:15], RRb[:, :, :, 0:15], CC, ALU.mult, ["RR", "CCY"], ["SCRa"])
            C.tt("dve", BT2[1][:, :, :, 0:15], IIb[:, :, :, 0:15], CC[:, ::-1], ALU.mult, ["II", "CCY"], ["SCRb"])
            C.tt("dve", CC, BT1[1][:, :, :, 0:15], BT2[1][:, :, :, 0:15], ALU.add, ["SCRa", "SCRb"], ["CCY"])
            C.tt("dve", Z5[:, :, :, 1:16, j], Z5[:, :, :, 1:16, j], CC, ALU.add, ["CCY", "Zr"], ["Zr"])
        if dbg and gh == 1:
            C.dbg("Z", Z, [128, 2, 16, 256], ["Zr"])
            C.dbg("UT", UT[1][:], [128, 16, 2, 256], [("UT", g) for g in range(16)])
            C.dbg("IN", IN[1][:], [128, 16, 2, 128], [("IN", g) for g in range(8)])
        C.memset("dve", SPB[1][:, :, 0:1], 0.0, ["SPB"])
        C.copy("dve", SPB[1][:, :, 1:256], Z[:, 0, :, 0:255], ["Zr"], ["SPB", "Zdone"])
        for ct in range(2):
            for ql in range(2):
                q = gh * 2 + ql
                i = ycnt["n"] % 2
                ycnt["n"] += 1
                C.tt("pool", YQ[i][1][:].rearrange("p r (g h) -> p r g h", g=8),
                     U[:, ct, q * 8:(q + 1) * 8, :, :].rearrange("p g r h -> p r g h"),
                     DBC[:, q * 128:(q + 1) * 128].rearrange("p (g h) -> p g h", g=8).unsqueeze(1).broadcast_to([128, 16, 8, 16]),
                     ALU.mult, [("U", ct), "DBC"], [YQ[i][0]])
                for j in range(4):
                    bank, bk = C.psum()
                    for gi in range(2):
                        g = ql * 8 + j * 2 + gi
                        o_ap = bank[:, gi * 256:(gi + 1) * 256]
                        for kt in range(2):
                            C.mm(o_ap, UT[1][:, g, kt, ct * 128:(ct + 1) * 128], TOEP[1][:, g, kt, :], kt == 0, False,
                                 [("UT", g), ("TOEP", g)], [bk])
                        C.mm(o_ap, SPB[1][:, g, ct * 128:(ct + 1) * 128], CA[1][:, g, :, :], False, True,
                             ["SPB", "CA"], [bk])
                    ysl = YQ[i][1][:, :, j * 32:(j + 1) * 32].rearrange("p r (g h) -> p g r h", g=2)
                    C.tt("dve", ysl, bank[:, :].rearrange("p (g r h) -> p g r h", g=2, r=16), ysl, ALU.add,
                         [bk, YQ[i][0]], [YQ[i][0]])
                C.act(X2[i][1][:], YQ[i][1][:], AF.Square, [YQ[i][0]], [X2[i][0]])
                C.ts("dve", X2[i][1][:], X2[i][1][:], 0.044715, ALU.mult, [X2[i][0]], [X2[i][0]], s2=1.0, op1=ALU.add)
                C.tt("pool", X2[i][1][:], X2[i][1][:], YQ[i][1][:], ALU.mult, [X2[i][0], YQ[i][0]], [X2[i][0]])
                C.act(X2[i][1][:], X2[i][1][:], AF.Sigmoid, [X2[i][0]], [X2[i][0]], scale=2.0 * math.sqrt(2.0 / math.pi))
                C.tt("pool", X2[i][1][:], X2[i][1][:], YQ[i][1][:], ALU.mult, [X2[i][0], YQ[i][0]], [X2[i][0]])
                if gT_out is None:
                    o = C.dma("sp", g_out.rearrange("(c r) ch -> c r ch", r=16)[ct * 128:(ct + 1) * 128, :, q * 128:(q + 1) * 128],
                              X2[i][1][:], reads=[X2[i][0]])
                    if final:
                        C.outs.append(o)
                else:
                    for r4 in range(4):
                        bank, bk = C.psum()
                        for rr in range(4):
                            C.tr(bank[:, rr * 128:(rr + 1) * 128], X2[i][1][:, r4 * 4 + rr, :], ID[:], [X2[i][0], "ID"], [bk])
                        C.copy("act" if r4 % 2 else "dve", GTS[:, :, r4 * 4:(r4 + 1) * 4].rearrange("p c r -> p r c"),
                               bank[:, :].rearrange("p (r c) -> p r c", r=4), [bk], ["GTS"])
                    C.dma("sp", gT_out[q * 128:(q + 1) * 128, ct * 2048:(ct + 1) * 2048],
                          WINB[:, :], reads=["GTS"], writes=[("gT_d", q, ct)])


_CACHE = {}


def _prog(name, fn):
    if name not in _CACHE:
        _CACHE[name] = fn()
    return _CACHE[name]


def _vec_layout(v, n):
    return np.ascontiguousarray(v.reshape(n, 128).T)


def _s5_inputs(x_b, w_in, lam_re, lam_im, b_re, b_im, c_re, c_im, d_skip, log_step, h):
    gs = slice(h * 32, (h + 1) * 32)
    rep = lambda a: np.ascontiguousarray(np.concatenate([a.T, a.T], axis=0))
    bre = b_re[gs].transpose(1, 0, 2)
    bim = b_im[gs].transpose(1, 0, 2)
    cre = c_re[gs].transpose(2, 0, 1)
    cim = c_im[gs].transpose(2, 0, 1)
    cm = np.zeros((128, 2, 256), np.float32)
    for kt in range(2):
        for rl in range(8):
            cm[rl * 16:(rl + 1) * 16, kt, (kt * 8 + rl) * 16:] = 1.0
    sign = np.concatenate([-np.ones((64, 1), np.float32), np.ones((64, 1), np.float32)], 0)
    return {"xT": np.ascontiguousarray(x_b.T), "win": np.ascontiguousarray(w_in[:, h * 512:(h + 1) * 512]),
            "lamre": rep(lam_re[gs]), "lamim": rep(lam_im[gs]),
            "logstep": np.ascontiguousarray(np.broadcast_to(log_step[gs][None, :], (128, 32))),
            "bst": np.ascontiguousarray(np.concatenate([bre, bim], 0)),
            "bsw": np.ascontiguousarray(np.concatenate([bim, bre], 0)),
            "cst": np.ascontiguousarray(np.concatenate([cre, cim], 0)),
            "csw": np.ascontiguousarray(np.concatenate([cim, cre], 0)),
            "dvec": np.ascontiguousarray(d_skip[h * 512:(h + 1) * 512][None, :]),
            "ident": np.eye(128, dtype=np.float32), "cmask": cm, "sign": sign}


def build_fused():
    C = Ctx()
    xT = C.din("xT", [D, SEQ])
    x = C.din("x", [SEQ, D])
    win = C.din("win", [D, D])
    lamre = C.din("lamre", [128, 64])
    lamim = C.din("lamim", [128, 64])
    logstep = C.din("logstep", [128, 64])
    bst = C.din("bst", [128, 64, 16])
    bsw = C.din("bsw", [128, 64, 16])
    cst = C.din("cst", [128, 64, 16])
    csw = C.din("csw", [128, 64, 16])
    dvec = C.din("dvec", [1, D])
    ident = C.din("ident", [128, 128])
    cmask = C.din("cmask", [128, 2, 256])
    sign = C.din("sign", [128, 1])
    masku = C.din("masku", [128, 128])
    wglu = C.din("wglu", [D, D])
    bglu = C.din("bglu", [128, 8])
    wout0 = C.din("wout0", [D, D])
    wout1 = C.din("wout1", [D, D])
    w1 = [C.din(f"w1_{l}", [D, DFF]) for l in range(2)]
    b1 = [C.din(f"b1_{l}", [128, 32]) for l in range(2)]
    w2 = [C.din(f"w2_{l}", [DFF, D]) for l in range(2)]
    vecs = [C.din(f"vecs_{l}", [5, D]) for l in range(2)]
    wq = C.din("wq", [D, D])
    wkv = C.din("wkv", [D, 2 * D])
    kidx = C.din("kidx", [128, 32], U32)
    qidx = C.din("qidx", [128, 16], U32)
    mask2 = C.din("mask2", [128, 256])
    out = C.dout("out", [SEQ // 2, D])
    gT_d = C.dscratch("gT_d", [D, SEQ])
    x1_d = C.dscratch("x1_d", [SEQ, D])
    x1own_d = C.dscratch("x1own_d", [SEQ // 2, D])
    oT_d = C.dscratch("oT_d", [D, SEQ // 2])
    C.init_psum()

    def bf16_copy(name, src, rows, cols):
        dst = C.dscratch(name + "_bf", [rows, cols], BF16)
        def issue():
            for r0 in range(0, rows, 1024):
                for c0 in range(0, cols, 2048):
                    c1 = min(cols, c0 + 2048)
                    C.dma("pool", dst[r0:r0 + 1024, c0:c1], src[r0:r0 + 1024, c0:c1], writes=[(name, r0, c0)])
        return dst, issue
    wglu, i1 = bf16_copy("wglu", wglu, D, D)
    wout0, i2 = bf16_copy("wout0", wout0, D, D)
    w1[0], i3 = bf16_copy("w1_0", w1[0], D, DFF)
    w2[0], i4 = bf16_copy("w2_0", w2[0], DFF, D)
    wq, i5 = bf16_copy("wq", wq, D, D)
    wkv, i6 = bf16_copy("wkv", wkv, D, 2 * D)
    wout1, i7 = bf16_copy("wout1", wout1, D, D)
    w1[1], i8 = bf16_copy("w1_1", w1[1], D, DFF)
    w2[1], i9 = bf16_copy("w2_1", w2[1], DFF, D)
    cast_batches = [lambda: [f_() for f_ in (i1, i2, i3, i4)], lambda: [f_() for f_ in (i5, i6, i7, i8, i9)]]
    for hh in range(2):
        gs = slice(hh * 32, (hh + 1) * 32)
        cs = slice(hh * 512, (hh + 1) * 512)
        T = {"xT": xT, "win": win[:, cs], "lamre": lamre[:, gs], "lamim": lamim[:, gs], "logstep": logstep[:, gs],
             "bst": bst[:, gs, :], "bsw": bsw[:, gs, :], "cst": cst[:, gs, :], "csw": csw[:, gs, :],
             "dvec": dvec[:, cs], "ident": ident, "cmask": cmask, "sign": sign}
        emit_s5(C, T, final=False, gT_out=gT_d[cs, :], after_uproj=cast_batches[hh])
        C.next_stage()
    TB = {"inT": gT_d, "xres": x, "wglu": wglu, "bglu": bglu, "wout": wout0, "w1": w1[0], "b1": b1[0], "w2": w2[0],
          "vecs": vecs[0], "ident": ident, "xout": x1_d}
    emit_post(C, TB, True, SEQ, 1024, final=False)
    C.next_stage()
    TC = {"x1": x1_d, "x1own": x1own_d, "kidx": kidx, "qidx": qidx, "mask2": mask2, "wq": wq, "wk": wkv[:, 0:D], "wv": wkv[:, D:2 * D],
          "ident": ident, "masku": masku, "oT": oT_d}
    emit_attn(C, TC, 8, final=False, half=True)
    C.next_stage()
    TD = {"inT": oT_d, "xres": x1own_d, "wout": wout1, "w1": w1[1], "b1": b1[1], "w2": w2[1], "vecs": vecs[1],
          "ident": ident, "xout": out}
    emit_post(C, TD, False, SEQ // 2, 1024, final=True)
    return C.finish()


def kernel(x, s5_w_in, s5_lambda_re, s5_lambda_im, s5_b_re, s5_b_im, s5_c_re, s5_c_im,
           s5_d, s5_log_step, s5_w_glu, s5_b_glu, s5_w_out,
           sb_w_kv, sb_w_q, sb_w_out,
           mlp_w1, mlp_b1, mlp_w2, mlp_b2,
           ln_mix_g, ln_mix_b, ln_mlp_g, ln_mlp_b):
    f = lambda a: np.ascontiguousarray(np.asarray(a, dtype=np.float32))
    x = f(x)
    rep = lambda a: np.ascontiguousarray(np.concatenate([a.T, a.T], axis=0))
    bre = f(s5_b_re)[0].transpose(1, 0, 2)
    bim = f(s5_b_im)[0].transpose(1, 0, 2)
    cre = f(s5_c_re)[0].transpose(2, 0, 1)
    cim = f(s5_c_im)[0].transpose(2, 0, 1)
    cm = np.zeros((128, 2, 256), np.float32)
    for kt in range(2):
        for rl in range(8):
            cm[rl * 16:(rl + 1) * 16, kt, (kt * 8 + rl) * 16:] = 1.0
    shared = {
        "win": f(s5_w_in)[0], "lamre": rep(f(s5_lambda_re)[0]), "lamim": rep(f(s5_lambda_im)[0]),
        "logstep": np.ascontiguousarray(np.broadcast_to(f(s5_log_step)[0][None, :], (128, 64))),
        "bst": np.ascontiguousarray(np.concatenate([bre, bim], 0)), "bsw": np.ascontiguousarray(np.concatenate([bim, bre], 0)),
        "cst": np.ascontiguousarray(np.concatenate([cre, cim], 0)), "csw": np.ascontiguousarray(np.concatenate([cim, cre], 0)),
        "dvec": f(s5_d)[0][None, :].copy(), "ident": np.eye(128, dtype=np.float32), "cmask": cm,
        "sign": np.concatenate([-np.ones((64, 1), np.float32), np.ones((64, 1), np.float32)], 0),
        "masku": np.triu(np.ones((128, 128), np.float32)),
        "wglu": f(s5_w_glu)[0], "bglu": _vec_layout(f(s5_b_glu)[0], 8), "wout0": f(s5_w_out)[0], "wout1": f(sb_w_out)[0],
        "wq": f(sb_w_q)[0], "wkv": f(sb_w_kv),
    }
    for l in range(2):
        shared[f"w1_{l}"] = f(mlp_w1)[l]
        shared[f"b1_{l}"] = _vec_layout(f(mlp_b1)[l], 32)
        shared[f"w2_{l}"] = f(mlp_w2)[l]
        shared[f"vecs_{l}"] = np.stack([f(ln_mix_g)[l], f(ln_mix_b)[l], f(mlp_b2)[l], f(ln_mlp_g)[l], f(ln_mlp_b)[l]])
    in_maps = []
    ar = np.arange(128, dtype=np.uint32)[:, None]
    tiles = np.arange(32, dtype=np.uint32)[None, :]
    for c in range(8):
        b, h = c // 2, c % 2
        m = dict(shared)
        m["xT"] = np.ascontiguousarray(x[b].T)
        m["x"] = x[b]
        m["kidx"] = np.ascontiguousarray((tiles * 128 + ar).astype(np.uint32))
        qblocks = 2 * np.arange(16, dtype=np.uint32)[None, :] + h
        m["qidx"] = np.ascontiguousarray((qblocks * 128 + ar).astype(np.uint32))
        tri = np.triu(np.ones((128, 128), np.float32))
        if h == 1:
            m["mask2"] = np.ascontiguousarray(np.concatenate([np.zeros((128, 128), np.float32), tri], 1))
        else:
            m["mask2"] = np.ascontiguousarray(np.concatenate([tri, np.ones((128, 128), np.float32)], 1))
        in_maps.append(m)
    res = run_bass_kernel_spmd(_prog("fused", build_fused), in_maps, core_ids=list(range(8)))
    out = np.empty((BATCH, SEQ, D), np.float32)
    for c in range(8):
        b, h = c // 2, c % 2
        r = res.results[c]["out"].reshape(16, 128, D)
        out[b].reshape(32, 128, D)[h::2] = r
    return out
```

```python
import contextlib
import math

import numpy as np

import concourse.bass as bass
import concourse.mybir as mybir
from concourse.bass_utils import run_bass_kernel_spmd

F32 = mybir.dt.float32
BF16 = mybir.dt.bfloat16
U32 = mybir.dt.uint32
AF = mybir.ActivationFunctionType
ALU = mybir.AluOpType
AX = mybir.AxisListType

D = 1024
DFF = 4096
SEQ = 4096
BATCH = 4
ALPHA = (2.0 * 2) ** 0.25
EPS = 1e-5

ENGS = ("pe", "act", "dve", "pool", "sp")
NDMA = 16


class _Op:
    __slots__ = ("eng", "fn", "waits", "is_dma", "needs_inc", "token", "dma_slot", "dma_round")

    def __init__(self, eng, fn, is_dma):
        self.eng = eng
        self.fn = fn
        self.is_dma = is_dma
        self.waits = []
        self.needs_inc = False
        self.token = None
        self.dma_slot = None
        self.dma_round = None


class Sched:
    def __init__(self, nc, same_engine_sync=True):
        self.nc = nc
        self.ops = {e: [] for e in ENGS}
        self.last_w = {}
        self.readers = {}
        self.dma_count = {e: 0 for e in ENGS}
        self.dma_hist = {e: [] for e in ENGS}
        self.same_engine_sync = same_engine_sync
        self.pending_barrier = {e: [] for e in ENGS}

    def barrier(self):
        tails = []
        for e in ENGS:
            comp = [o for o in self.ops[e] if not o.is_dma]
            if comp:
                tails.append(comp[-1])
            tails.extend(self.dma_hist[e][-NDMA:])
        for e in ENGS:
            self.pending_barrier[e] = list(tails)

    def op(self, eng, fn, reads=(), writes=(), dma=False):
        o = _Op(eng, fn, dma)
        deps = []
        force = self.pending_barrier[eng]
        self.pending_barrier[eng] = []
        for k in reads:
            w = self.last_w.get(k)
            if w is not None:
                deps.append(w)
        for k in writes:
            w = self.last_w.get(k)
            if w is not None:
                deps.append(w)
            deps.extend(self.readers.get(k, ()))
        if dma:
            n = self.dma_count[eng]
            o.dma_slot = n % NDMA
            o.dma_round = n // NDMA
            self.dma_count[eng] = n + 1
            hist = self.dma_hist[eng]
            if n >= NDMA:
                deps.append(hist[n - NDMA])
            hist.append(o)
            o.needs_inc = True
        seen = set()
        for d in force:
            if id(d) in seen:
                continue
            seen.add(id(d))
            if d.eng == eng and not d.is_dma:
                continue
            d.needs_inc = True
            o.waits.append(d)
        for d in deps:
            if d is o or id(d) in seen:
                continue
            seen.add(id(d))
            if d.eng == eng and not d.is_dma and not dma:
                if eng == "pe" or not self.same_engine_sync:
                    continue
            d.needs_inc = True
            o.waits.append(d)
        for k in reads:
            self.readers.setdefault(k, []).append(o)
        for k in writes:
            self.last_w[k] = o
            self.readers[k] = []
        self.ops[eng].append(o)
        return o

    def emit(self, final_wait_ops=()):
        nc = self.nc
        with contextlib.ExitStack() as st:
            esem = {e: st.enter_context(nc.semaphore(f"s_{e}")) for e in ENGS}
            dsem = {e: [st.enter_context(nc.semaphore(f"d_{e}{i}")) for i in range(NDMA)]
                    for e in ENGS if self.dma_count[e] > 0}
            for e in ENGS:
                c = 0
                for o in self.ops[e]:
                    if o.is_dma:
                        o.token = (dsem[e][o.dma_slot], 16 * (o.dma_round + 1))
                    elif o.needs_inc:
                        c += 1
                        o.token = (esem[e], c)
            block = st.enter_context(nc.Block())

            def run(e):
                def body(eng):
                    waited = {}
                    for o in self.ops[e]:
                        for d in o.waits:
                            sem, val = d.token
                            key = id(sem)
                            if waited.get(key, 0) >= val:
                                continue
                            waited[key] = val
                            eng.wait_ge(sem, val)
                        ins = o.fn(eng)
                        if o.is_dma:
                            ins.then_inc(o.token[0], 16)
                        elif o.needs_inc:
                            ins.then_inc(o.token[0], 1)
                    if e == "sp":
                        for d in final_wait_ops:
                            sem, val = d.token
                            eng.wait_ge(sem, val)
                return body

            block.tensor(run("pe"))
            block.scalar(run("act"))
            block.vector(run("dve"))
            block.gpsimd(run("pool"))
            block.sync(run("sp"))


class Ctx:
    def __init__(self, same_engine_sync=True):
        self.nc = bass.Bass("TRN2", target_bir_lowering=False)
        self.S = Sched(self.nc, same_engine_sync)
        self.st = contextlib.ExitStack()
        self.pst = contextlib.ExitStack()
        self.psn = 0
        self.banks = None
        self.outs = []
        self.ring = 8
        self.stage = 0

    def din(self, name, shape, dt=F32):
        return self.nc.dram_tensor(name, list(shape), dt, kind="ExternalInput").ap()

    def dout(self, name, shape, dt=F32):
        return self.nc.dram_tensor(name, list(shape), dt, kind="ExternalOutput").ap()

    def sb(self, name, shape, dt=F32):
        return self.st.enter_context(self.nc.sbuf_tensor(f"{name}_s{self.stage}", list(shape), dt))

    def init_psum(self):
        self.banks = [self.pst.enter_context(self.nc.psum_tensor(f"ps{i}", [128, 512], F32)) for i in range(8)]

    def psum(self):
        i = self.psn % self.ring
        self.psn += 1
        return self.banks[i], ("ps", i)

    def dma(self, q, out, in_, reads=(), writes=(), **kw):
        return self.S.op(q, lambda e: e.dma_start(out=out, in_=in_, **kw), reads=reads, writes=writes, dma=True)

    def wload(self, out, in_, writes, alt=0):
        if in_.dtype == BF16:
            return self.dma("sp" if alt % 2 == 0 else "act", out, in_, writes=writes)
        return self.dma("pool", out, in_, writes=writes)

    def mm(self, out, lhsT, rhs, start, stop, reads, writes):
        return self.S.op("pe", lambda e: e.matmul(out, lhsT=lhsT, rhs=rhs, start=start, stop=stop),
                         reads=reads, writes=writes)

    def tr(self, out, in_, ident, reads, writes):
        return self.S.op("pe", lambda e: e.transpose(out, in_, ident), reads=reads, writes=writes)

    def act(self, out, in_, func, reads, writes, bias=None, scale=None):
        kw = {}
        if bias is not None:
            kw["bias"] = bias
        if scale is not None:
            kw["scale"] = scale
        return self.S.op("act", lambda e: e.activation(out=out, in_=in_, func=func, **kw), reads=reads, writes=writes)

    def tt(self, eng, out, in0, in1, op, reads, writes):
        return self.S.op(eng, lambda e: e.tensor_tensor(out=out, in0=in0, in1=in1, op=op), reads=reads, writes=writes)

    def ts(self, eng, out, in0, s1, op0, reads, writes, s2=None, op1=None):
        if op1 is None:
            return self.S.op(eng, lambda e: e.tensor_scalar(out=out, in0=in0, scalar1=s1, scalar2=None, op0=op0),
                             reads=reads, writes=writes)
        return self.S.op(eng, lambda e: e.tensor_scalar(out=out, in0=in0, scalar1=s1, scalar2=s2, op0=op0, op1=op1),
                         reads=reads, writes=writes)

    def stt(self, out, in0, scalar, in1, op0, op1, reads, writes):
        return self.S.op("dve", lambda e: e.scalar_tensor_tensor(out=out, in0=in0, scalar=scalar, in1=in1,
                                                                op0=op0, op1=op1), reads=reads, writes=writes)

    def copy(self, eng, out, in_, reads, writes):
        if eng == "act":
            return self.act(out, in_, AF.Identity, reads, writes)
        return self.S.op(eng, lambda e: e.tensor_copy(out=out, in_=in_), reads=reads, writes=writes)

    def memset(self, eng, ap, val, writes):
        return self.S.op(eng, lambda e: e.memset(ap, val), writes=writes)

    def dbg(self, name, ap, shape, keys):
        o = self.dout("dbg_" + name, shape)
        self.outs.append(self.dma("pool", o, ap, reads=keys))

    def dscratch(self, name, shape, dt=F32):
        return self.nc.dram_tensor(name, list(shape), dt, kind="Internal").ap()

    def next_stage(self):
        self.S.barrier()
        self.st.close()
        self.st = contextlib.ExitStack()
        self.stage += 1

    def finish(self):
        self.S.emit(final_wait_ops=self.outs)
        self.st.close()
        self.pst.close()
        return self.nc


def build_post(glu, NT=2048, PT=1024):
    C = Ctx()
    T = {"inT": C.din("inT", [D, NT]), "xres": C.din("xres", [NT, D])}
    if glu:
        T["wglu"] = C.din("wglu", [D, D])
        T["bglu"] = C.din("bglu", [128, 8])
    T["wout"] = C.din("wout", [D, D])
    T["w1"] = C.din("w1", [D, DFF])
    T["b1"] = C.din("b1", [128, 32])
    T["w2"] = C.din("w2", [DFF, D])
    T["vecs"] = C.din("vecs", [5, D])
    T["ident"] = C.din("ident", [128, 128])
    T["xout"] = C.dout("xout", [NT, D])
    C.init_psum()
    emit_post(C, T, glu, NT, PT, final=True)
    return C.finish()


def emit_post(C, T, glu, NT, PT, final, xT_out=None):
    dbg = False
    NTILE = PT // 128
    NCH = PT // 512
    inT, xres = T["inT"], T["xres"]
    if glu:
        wglu, bglu = T["wglu"], T["bglu"]
    wout, w1, b1, w2, vecs, ident, xout = T["wout"], T["w1"], T["b1"], T["w2"], T["vecs"], T["ident"], T["xout"]
    C.ring = 8

    R = C.sb("R", [128, NTILE, D], F32)
    AT0 = C.sb("AT0", [128, 8, PT], BF16)
    AT1 = C.sb("AT1", [128, 8, PT], BF16) if glu else None
    HT = [C.sb(f"HT{i}", [128, 4, PT], BF16) for i in range(2)]
    WO = C.sb("WO", [128, 8, D], BF16)
    W1T = [C.sb(f"W1T{i}", [128, 8, 512], BF16) for i in range(2)]
    W2T = [C.sb(f"W2T{i}", [128, 4, D], BF16) for i in range(2)]
    VB = C.sb("VB", [128, 5, D], F32)
    B1 = C.sb("B1", [128, 32], F32)
    ID = C.sb("ID", [128, 128], F32)
    RELU = [C.sb(f"RELU{i}", [128, 512], F32) for i in range(2)]
    XH = [C.sb(f"XH{i}", [128, D], F32) for i in range(2)]
    ST = [C.sb(f"ST{i}", [128, 2, 6], F32) for i in range(2)]
    MV = [C.sb(f"MV{i}", [128, 2], F32) for i in range(2)]
    RS = [C.sb(f"RS{i}", [128, 1], F32) for i in range(2)]
    NMR = [C.sb(f"NMR{i}", [128, 1], F32) for i in range(2)]
    if glu:
        WG = [C.sb(f"WG{i}", [128, 8, 128], BF16) for i in range(2)]
        GF = [C.sb(f"GF{i}", [128, PT], F32) for i in range(2)]
        SIG = [C.sb(f"SIG{i}", [128, 512], F32) for i in range(2)]
        BG = C.sb("BG", [128, 8], F32)
    XTS = C.sb("XTS", [128, 8, PT], BF16) if xT_out is not None else None

    C.dma("sp", ID[:], ident, writes=["ID"])
    C.dma("sp", B1[:], b1, writes=["B1"])
    if glu:
        C.dma("sp", BG[:], bglu, writes=["BG"])
    for i in range(5):
        C.dma("sp", VB[:, i, :], vecs[i:i + 1, :].partition_broadcast(128), writes=[("VB", i)])
    C.wload(WO[:], wout.rearrange("(k p) n -> p k n", p=128), ["WO"])

    inT_v = inT.rearrange("(k p) t -> p k t", p=128)
    w1_v = w1.rearrange("(k p) n -> p k n", p=128)
    cnt = {"ln": 0}

    def layernorm(t, gi, bi):
        i = cnt["ln"] % 2
        cnt["ln"] += 1
        rk = ("R", t)
        C.S.op("dve", lambda e: e.bn_stats(out=ST[i][:, 0, :], in_=R[:, t, 0:512]), reads=[rk], writes=[("ST", i)])
        C.S.op("dve", lambda e: e.bn_stats(out=ST[i][:, 1, :], in_=R[:, t, 512:1024]), reads=[rk], writes=[("ST", i)])
        C.S.op("dve", lambda e: e.bn_aggr(out=MV[i][:], in_=ST[i][:].rearrange("p a b -> p (a b)")),
               reads=[("ST", i)], writes=[("MV", i)])
        C.act(RS[i][:], MV[i][:, 1:2], AF.Sqrt, [("MV", i)], [("RS", i)], bias=EPS)
        C.S.op("dve", lambda e: e.reciprocal(out=RS[i][:], in_=RS[i][:]), reads=[("RS", i)], writes=[("RS", i)])
        C.stt(NMR[i][:], MV[i][:, 0:1], -1.0, RS[i][:], ALU.mult, ALU.mult, [("MV", i), ("RS", i)], [("NMR", i)])
        C.act(XH[i][:], R[:, t, :], AF.Identity, [rk, ("RS", i), ("NMR", i)], [("XH", i)], bias=NMR[i][:], scale=RS[i][:])
        C.tt("pool", XH[i][:], XH[i][:], VB[:, gi, :], ALU.mult, [("XH", i), ("VB", gi)], [("XH", i)])
        C.tt("pool", R[:, t, :], XH[i][:], VB[:, bi, :], ALU.add, [("XH", i), ("VB", bi)], [rk])

    for p in range(NT // PT):
        tok0 = p * PT
        C.dma("pool", AT0[:], inT_v[:, :, tok0:tok0 + PT], writes=[("AT0", t) for t in range(NTILE)])
        C.dma("sp", R[:], xres[tok0:tok0 + PT, :].rearrange("(t p) d -> p t d", p=128),
              writes=[("R", t) for t in range(NTILE)])
        at0_all = [("AT0", t) for t in range(NTILE)]
        if glu:
            def load_wg(m):
                C.wload(WG[m % 2][:], wglu.rearrange("(k p) n -> p k n", p=128)[:, :, m * 128:(m + 1) * 128],
                        [("WG", m % 2)])
                C.dma("sp", GF[m % 2][:], inT[m * 128:(m + 1) * 128, tok0:tok0 + PT], writes=[("GF", m % 2)])
            load_wg(0)
            for m in range(8):
                if m + 1 < 8:
                    load_wg(m + 1)
                for c in range(NCH):
                    bank, bk = C.psum()
                    for k in range(8):
                        C.mm(bank[:, :], WG[m % 2][:, k, :], AT0[:, k, c * 512:(c + 1) * 512], k == 0, k == 7,
                             [("WG", m % 2)] + at0_all, [bk])
                    si = (m * NCH + c) % 2
                    C.act(SIG[si][:], bank[:, :], AF.Sigmoid, [bk, "BG"], [("SIG", si)], bias=BG[:, m:m + 1])
                    C.tt("dve", AT1[:, m, c * 512:(c + 1) * 512], GF[m % 2][:, c * 512:(c + 1) * 512], SIG[si][:],
                         ALU.mult, [("GF", m % 2), ("SIG", si)], [("AT1", m, c)])
            src = AT1
            if dbg:
                C.outs.append(C.dma("pool", dbg_glu.rearrange("(k p) t -> p k t", p=128)[:, :, tok0:tok0 + PT], AT1[:], reads=[("AT1", m, c) for m in range(8) for c in range(NCH)]))
            src_keys = lambda t: [("AT1", m, t // 4) for m in range(8)]
        else:
            src = AT0
            src_keys = lambda t: [("AT0", t)]

        def outproj(t):
            for n in range(2):
                bank, bk = C.psum()
                for k in range(8):
                    C.mm(bank[:, :], src[:, k, t * 128:(t + 1) * 128], WO[:, k, n * 512:(n + 1) * 512], k == 0, k == 7,
                         ["WO"] + src_keys(t), [bk])
                C.stt(R[:, t, n * 512:(n + 1) * 512], R[:, t, n * 512:(n + 1) * 512], ALPHA, bank[:, :],
                      ALU.mult, ALU.add, [("R", t), bk], [("R", t)])
            layernorm(t, 0, 1)

        def transposes(t):
            for h in range(2):
                bank, bk = C.psum()
                for kk in range(4):
                    k = 4 * h + kk
                    C.tr(bank[:, kk * 128:(kk + 1) * 128], R[:, t, k * 128:(k + 1) * 128], ID[:], [("R", t), "ID"], [bk])
                C.act(AT0[:, 4 * h:4 * h + 4, t * 128:(t + 1) * 128], bank[:, :].rearrange("p (k c) -> p k c", k=4),
                      AF.Identity, [bk], [("AT0", t)])
            if dbg:
                C.outs.append(C.dma("sp", dbg_xa[tok0 + t * 128: tok0 + (t + 1) * 128, :], R[:, t, :], reads=[("R", t)]))
            C.stt(R[:, t, :], R[:, t, :], ALPHA, VB[:, 2, :], ALU.mult, ALU.add, [("R", t), ("VB", 2)], [("R", t)])

        for t in range(NTILE + 1):
            if t < NTILE:
                outproj(t)
            if t >= 1:
                transposes(t - 1)

        NJB = DFF // 512

        def load_w1(jb):
            C.wload(W1T[jb % 2][:], w1_v[:, :, jb * 512:(jb + 1) * 512], [("W1T", jb % 2)], alt=0)

        def load_w2(jb):
            C.wload(W2T[jb % 2][:], w2[jb * 512:(jb + 1) * 512, :].rearrange("(j p) n -> p j n", p=128),
                    [("W2T", jb % 2)], alt=1)

        def mlp1(jb):
            b = jb % 2
            for jj in range(4):
                j = jb * 4 + jj
                for c in range(NCH):
                    bank, bk = C.psum()
                    for k in range(8):
                        C.mm(bank[:, :], W1T[b][:, k, jj * 128:(jj + 1) * 128], AT0[:, k, c * 512:(c + 1) * 512],
                             k == 0, k == 7, [("W1T", b)] + at0_all, [bk])
                    ri = (jj * NCH + c) % 2
                    C.act(RELU[ri][:], bank[:, :], AF.Relu, [bk, "B1"], [("RELU", ri)], bias=B1[:, j:j + 1])
                    C.stt(HT[b][:, jj, c * 512:(c + 1) * 512], bank[:, :], B1[:, j:j + 1], RELU[ri][:], ALU.add, ALU.mult,
                          [bk, "B1", ("RELU", ri)], [("HT", b, c)])

        def mlp2(jb):
            b = jb % 2
            for t in range(NTILE):
                for n in range(2):
                    bank, bk = C.psum()
                    for jj in range(4):
                        C.mm(bank[:, :], HT[b][:, jj, t * 128:(t + 1) * 128], W2T[b][:, jj, n * 512:(n + 1) * 512],
                             jj == 0, jj == 3, [("W2T", b), ("HT", b, t // 4)], [bk])
                    C.tt("dve", R[:, t, n * 512:(n + 1) * 512], bank[:, :], R[:, t, n * 512:(n + 1) * 512], ALU.add,
                         [bk, ("R", t)], [("R", t)])

        load_w1(0)
        load_w2(0)
        for jb in range(NJB + 1):
            if jb + 1 < NJB:
                load_w1(jb + 1)
            if jb < NJB:
                mlp1(jb)
            if jb >= 1:
                mlp2(jb - 1)
            if jb + 1 < NJB:
                load_w2(jb + 1)

        for t in range(NTILE):
            layernorm(t, 3, 4)
            o = C.dma("sp", xout[tok0 + t * 128: tok0 + (t + 1) * 128, :], R[:, t, :], reads=[("R", t)],
                      writes=[("xout", (tok0 + t * 128) // 512)])
            if final:
                C.outs.append(o)
            if xT_out is not None:
                for h in range(2):
                    bank, bk = C.psum()
                    for kk in range(4):
                        k = 4 * h + kk
                        C.tr(bank[:, kk * 128:(kk + 1) * 128], R[:, t, k * 128:(k + 1) * 128], ID[:], [("R", t), "ID"], [bk])
                    C.act(XTS[:, 4 * h:4 * h + 4, t * 128:(t + 1) * 128], bank[:, :].rearrange("p (k c) -> p k c", k=4),
                          AF.Identity, [bk], [("XTS", t)])
        if xT_out is not None:
            C.dma("sp", xT_out.rearrange("(k p) t -> p k t", p=128)[:, :, tok0:tok0 + PT], XTS[:],
                  reads=[("XTS", t) for t in range(NTILE)], writes=[("xT_out", tok0 // 1024)])


def build_attn():
    C = Ctx()
    T = {"x1T": C.din("x1T", [D, SEQ]), "wq": C.din("wq", [D, 512]), "wk": C.din("wk", [D, 512]),
         "wv": C.din("wv", [D, 512]), "ident": C.din("ident", [128, 128]), "masku": C.din("masku", [128, 128]),
         "oT": C.dout("oT", [512, SEQ])}
    C.init_psum()
    emit_attn(C, T, 4, final=True)
    return C.finish()


ATT_NB = 5
ATT_NPT = 10
ATT_LAGS = (0, 2, 3, 5, 7)


def emit_attn(C, T, NPAIR, final, x_bf16=False, half=False):
    C.ring = 6
    wq, wk, wv, ident, masku, oT = T["wq"], T["wk"], T["wv"], T["ident"], T["masku"], T["oT"]
    x1T = T.get("x1T")
    QLEN = 2048 if half else SEQ

    XT = C.sb("XT", [128, 8, SEQ], BF16)
    QT = [C.sb(f"QT{i}", [128, QLEN], BF16) for i in range(2)]
    KT = [C.sb(f"KT{i}", [128, SEQ], BF16) for i in range(2)]
    V = [C.sb(f"V{i}", [128, 32, 128], BF16) for i in range(2)]
    WQ = [C.sb(f"WQ{i}", [128, 8, 128], BF16) for i in range(2)]
    WK = [C.sb(f"WK{i}", [128, 8, 128], BF16) for i in range(2)]
    WV = [C.sb(f"WV{i}", [128, 8, 128], BF16) for i in range(2)]
    NB = ATT_NB
    NP8 = ATT_NPT
    PT = [C.sb(f"PT{i}", [128, 516], F32) for i in range(NP8)]
    OMB = [C.sb(f"OMB{i}", [128, 516], F32) for i in range(NB)]
    WW = [C.sb(f"WW{i}", [128, 512], BF16) for i in range(NB)]
    WT = [C.sb(f"WT{i}", [128, 512], BF16) for i in range(NB)]
    OS = [C.sb(f"OS{i}", [128, 128], F32) for i in range(4)]
    ONE = C.sb("ONE", [128, 1], F32)
    ZERO = C.sb("ZERO", [128, 516], F32)
    ID = C.sb("ID", [128, 128], F32)
    IDB = C.sb("IDB", [128, 128], BF16)
    MASKU = C.sb("MASKU", [128, 128], F32)

    C.dma("sp", ID[:], ident, writes=["ID"])
    C.dma("sp", MASKU[:], masku, writes=["MASKU"])
    C.copy("dve", IDB[:], ID[:], ["ID"], ["IDB"])
    C.memset("pool", ZERO[:], 0.0, ["ZERO"])
    C.memset("pool", ONE[:], 1.0, ["ONE"])
    for j_ in range(NB):
        C.memset("pool", OMB[j_][:, 512:513], 1.0, [("OMB", j_)])
    C.memset("pool", ZERO[:], 0.0, ["ZERO"])
    if not half:
        x_v = x1T.rearrange("(k p) t -> p k t", p=128)
        for c in range(4):
            C.dma("sp" if x_bf16 else "pool", XT[:, :, c * 1024:(c + 1) * 1024], x_v[:, :, c * 1024:(c + 1) * 1024],
                  reads=[("xT_out", c)], writes=[("XT", c)])
    else:
        KI = C.sb("KI", [128, 32], U32)
        QI = C.sb("QI", [128, 16], U32)
        M2 = C.sb("M2", [128, 256], BF16)
        XQ = C.sb("XQ", [128, 8, QLEN], BF16)
        NG = 3
        G = [C.sb(f"G{i}", [128, D], F32) for i in range(NG)]
        C.dma("sp", KI[:], T["kidx"], writes=["KI"])
        C.dma("sp", QI[:], T["qidx"], writes=["QI"])
        C.dma("pool", M2[:], T["mask2"], writes=["M2"])
        C.ts("dve", M2[:], M2[:], -30000.0, ALU.mult, ["M2"], ["M2"])
        gcnt = {"n": 0}

        def gather(idx_ap, dst, j, key, own):
            gi = gcnt["n"] % NG
            gcnt["n"] += 1
            g = G[gi]
            if own:
                C.S.op("pool", lambda e: e.indirect_dma_start(out=g[:], out_offset=None, in_=T["x1"],
                                                              in_offset=bass.IndirectOffsetOnAxis(ap=idx_ap, axis=0)),
                       reads=["KI", "QI"], writes=[("G", gi)], dma=True)
            else:
                C.dma("sp", g[:], T["x1"][j * 128:(j + 1) * 128, :], writes=[("G", gi)])
            if own:
                C.dma("sp", T["x1own"][j * 128:(j + 1) * 128, :], g[:], reads=[("G", gi)], writes=[("x1own", j)])
            for h in range(2):
                bank, bk = C.psum()
                for kk in range(4):
                    k = 4 * h + kk
                    C.tr(bank[:, kk * 128:(kk + 1) * 128], g[:, k * 128:(k + 1) * 128], ID[:], [("G", gi), "ID"], [bk])
                C.act(dst[:, 4 * h:4 * h + 4, j * 128:(j + 1) * 128], bank[:, :].rearrange("p (k c) -> p k c", k=4),
                      AF.Identity, [bk], [key])

        for j in range(32):
            gather(KI[:, j:j + 1], XT, j, ("XT", j // 8), False)
        for j in range(16):
            gather(QI[:, j:j + 1], XQ, j, ("XQ", j // 4), True)

    def load_w(pc):
        b = pc % 2
        for W, src, nm in ((WQ, wq, "WQ"), (WK, wk, "WK"), (WV, wv, "WV")):
            C.wload(W[b][:], src.rearrange("(k p) n -> p k n", p=128)[:, :, pc * 128:(pc + 1) * 128], [(nm, b)])

    def project(pc):
        b = pc % 2
        for c in range(8):
            for W, dst, nm, dnm, sc in ((WQ, QT, "WQ", "QT", 0.125), (WK, KT, "WK", "KT", None)):
                if sc is not None and c * 512 >= QLEN:
                    continue
                src, skey = (XQ, ("XQ", c)) if (half and sc is not None) else (XT, ("XT", c // 2))
                bank, bk = C.psum()
                for k in range(8):
                    C.mm(bank[:, :], W[b][:, k, :], src[:, k, c * 512:(c + 1) * 512], k == 0, k == 7,
                         [(nm, b), skey], [bk])
                if sc is not None:
                    C.act(dst[b][:, c * 512:(c + 1) * 512], bank[:, :], AF.Identity, [bk], [(dnm, b, c)], scale=sc)
                else:
                    C.copy("dve", dst[b][:, c * 512:(c + 1) * 512], bank[:, :], [bk], [(dnm, b, c)])
        for t4 in range(8):
            bank, bk = C.psum()
            for tt_ in range(4):
                t = t4 * 4 + tt_
                for k in range(8):
                    C.mm(bank[:, tt_ * 128:(tt_ + 1) * 128], XT[:, k, t * 128:(t + 1) * 128], WV[b][:, k, :],
                         k == 0, k == 7, [("WV", b), ("XT", t // 8)], [bk])
            C.copy("act" if t4 % 2 else "dve", V[b][:, t4 * 4:(t4 + 1) * 4, :],
                   bank[:, :].rearrange("p (t c) -> p t c", t=4), [bk], [("V", b, t4)])

    def steps_for(pc):
        st = []
        for i in range(QLEN // 128):
            top = gblk(i) // 4
            for kt in range(top, -1, -1):
                for hl in range(2):
                    st.append((pc, hl, i, kt, top))
        return st

    def gblk(i):
        return 2 * i + 1 if half else i

    cnt = {"q": 0}
    obank = {}

    def width(s):
        pc, hl, i, kt, top = s
        return ((gblk(i) % 4) + 1) * 128 if kt == top else 512

    def stA(n, s):
        pc, hl, i, kt, top = s
        b = pc % 2
        j = n % NB
        W = width(s)
        s0 = kt * 512
        base = 64 * hl
        if kt == top:
            obank[(pc, hl, i)] = 6 + hl
        zb, zk = C.psum()
        pe_mask = half and kt == top
        C.mm(zb[:, :W], QT[b][base:base + 64, i * 128:(i + 1) * 128], KT[b][base:base + 64, s0:s0 + W], True, not pe_mask,
             [("QT", b, i // 4), ("KT", b, kt)], [zk])
        if pe_mask:
            C.mm(zb[:, W - 256:W], IDB[:], M2[:], False, True, ["IDB", "M2"], [zk])
        C.act(OMB[j][:, :W], zb[:, :W], AF.Sigmoid, [zk], [("OMB", j)], scale=-1.0)
        if kt == top and not half:
            C.tt("dve", OMB[j][:, W - 128:W], OMB[j][:, W - 128:W], MASKU[:], ALU.max, [("OMB", j), "MASKU"], [("OMB", j)])

    def stB(n, s):
        pc, hl, i, kt, top = s
        j = n % NB
        p8 = n % NP8
        W = width(s)
        if kt == top:
            C.memset("pool", PT[p8][:, W:W + 1], 1.0, [("PT", p8)])
            C.S.op("dve", lambda e: e.tensor_tensor_scan(out=PT[p8][:, 0:W][:, ::-1], data0=OMB[j][:, 0:W][:, ::-1],
                                                        data1=ZERO[:, 0:W], initial=1.0, op0=ALU.mult, op1=ALU.add),
                   reads=[("OMB", j), "ZERO"], writes=[("PT", p8)])
        else:
            carry, ck = PT[(n - 2) % NP8][:, 0:1], ("PT", (n - 2) % NP8)
            C.S.op("dve", lambda e: e.tensor_tensor_scan(out=PT[p8][:, 0:513][:, ::-1], data0=OMB[j][:, 0:513][:, ::-1],
                                                        data1=ZERO[:, 0:513], initial=carry, op0=ALU.mult, op1=ALU.add),
                   reads=[("OMB", j), "ZERO", ck], writes=[("PT", p8)])

    def stB2(n, s):
        j = n % NB
        p8 = n % NP8
        W = width(s)
        C.tt("pool", WW[j][:, 0:W], PT[p8][:, 1:W + 1], PT[p8][:, 0:W], ALU.subtract, [("PT", p8)], [("WW", j)])

    def stC(n, s):
        j = n % NB
        W = width(s)
        tb, tk = C.psum()
        for blk in range(W // 128):
            C.mm(tb[:, blk * 128:(blk + 1) * 128], WW[j][:, blk * 128:(blk + 1) * 128], IDB[:], True, True,
                 [("WW", j), "IDB"], [tk])
        C.copy("act", WT[j][:, :W], tb[:, :W], [tk], [("WT", j)])

    def stD(n, s):
        pc, hl, i, kt, top = s
        b = pc % 2
        j = n % NB
        W = width(s)
        ob = C.banks[obank[(pc, hl, i)]]
        obk = ("ps", obank[(pc, hl, i)])
        nb = W // 128
        for blk in range(nb):
            stile = kt * 4 + blk
            C.mm(ob[0:64, 0:128], V[b][:, stile, hl * 64:(hl + 1) * 64], WT[j][:, blk * 128:(blk + 1) * 128],
                 kt == top and blk == 0, kt == 0 and blk == nb - 1, [("V", b, stile // 4), ("WT", j)], [obk])
        if kt == 0:
            oi = cnt["q"] % 4
            cnt["q"] += 1
            C.copy("act", OS[oi][0:64, :], ob[0:64, 0:128], [obk], [("OS", oi)])
            h = 2 * pc + hl
            o = C.dma("sp", oT[h * 64:(h + 1) * 64, i * 128:(i + 1) * 128], OS[oi][0:64, :], reads=[("OS", oi)],
                      writes=[("oT_d", h, i)])
            if final:
                C.outs.append(o)

    stages = tuple(zip((stA, stB, stB2, stC, stD), ATT_LAGS))
    load_w(0)
    project(0)
    for pc in range(NPAIR):
        if pc + 1 < NPAIR:
            load_w(pc + 1)
        steps = steps_for(pc)
        N = len(steps)
        maxlag = max(l for _, l in stages)
        for it in range(N + maxlag):
            for fn, lag in stages:
                idx = it - lag
                if 0 <= idx < N:
                    fn(idx, steps[idx])
            if it == N // 2 and pc + 1 < NPAIR:
                project(pc + 1)


GH_ORDER = (0, 1)


def build_s5():
    C = Ctx()
    T = {"xT": C.din("xT", [D, SEQ]), "win": C.din("win", [D, 512]), "lamre": C.din("lamre", [128, 32]),
         "lamim": C.din("lamim", [128, 32]), "logstep": C.din("logstep", [128, 32]),
         "bst": C.din("bst", [128, 32, 16]), "bsw": C.din("bsw", [128, 32, 16]),
         "cst": C.din("cst", [128, 32, 16]), "csw": C.din("csw", [128, 32, 16]),
         "dvec": C.din("dvec", [1, 512]), "ident": C.din("ident", [128, 128]),
         "cmask": C.din("cmask", [128, 2, 256]), "sign": C.din("sign", [128, 1]),
         "g_out": C.dout("g_out", [SEQ, 512])}
    C.init_psum()
    emit_s5(C, T, final=True)
    return C.finish()


def emit_s5(C, T, final, gT_out=None, after_uproj=None):
    dbg = False
    C.ring = 8
    xT, win, lamre, lamim, logstep = T["xT"], T["win"], T["lamre"], T["lamim"], T["logstep"]
    bst, bsw, cst, csw, dvec = T["bst"], T["bsw"], T["cst"], T["csw"], T["dvec"]
    ident, cmask, sign = T["ident"], T["cmask"], T["sign"]
    g_out = T.get("g_out")

    XTZ = C.sb("XTZ", [128, 8192], F32)
    XT = XTZ[:, :].bitcast(BF16).rearrange("p (k t) -> p k t", k=8)
    Z = XTZ[:, :].rearrange("p (s g c) -> p s g c", s=2, g=16)
    WINB = C.sb("WINB", [128, 2048], F32)
    WIN = WINB[:, :].bitcast(BF16).rearrange("p (k n) -> p k n", k=8)
    U = C.sb("U", [128, 2, 32, 16, 16], BF16)
    ID = C.sb("ID", [128, 128], F32)
    IDB = C.sb("IDB", [128, 128], BF16)
    CM = C.sb("CM", [128, 2, 256], F32)
    SG = C.sb("SGN", [128, 1], F32)
    NSG = C.sb("NSG", [128, 1], F32)
    DBC = C.sb("DBC", [128, 512], F32)
    BST = C.sb("BST", [128, 16, 16], F32)
    BSW = C.sb("BSW", [128, 16, 16], F32)
    CST = C.sb("CST", [128, 16, 16], F32)
    CSW = C.sb("CSW", [128, 16, 16], F32)
    GTS = WINB[:, :].rearrange("p (c r) -> p c r", r=16) if gT_out is not None else None

    def pt(name):
        return (name, C.sb(name, [128, 32], F32))

    def v2(out, a, b, op, eng="dve"):
        C.tt(eng, out[1][:], a[1][:], b[1][:], op, [a[0], b[0]], [out[0]])

    def vs(out, a, s1, op0, s2=None, op1=None, eng="dve"):
        C.ts(eng, out[1][:], a[1][:], s1, op0, [a[0]], [out[0]], s2=s2, op1=op1)

    def vact(out, a, func, scale=None, bias=None):
        C.act(out[1][:], a[1][:], func, [a[0]], [out[0]], bias=bias, scale=scale)

    C.dma("sp", ID[:], ident, writes=["ID"])
    C.copy("dve", IDB[:], ID[:], ["ID"], ["IDB"])
    C.dma("sp", CM[:], cmask, writes=["CM"])
    C.dma("sp", SG[:], sign, writes=["SGN"])
    C.ts("dve", NSG[:], SG[:], -1.0, ALU.mult, ["SGN"], ["NSG"])
    C.dma("sp", DBC[:], dvec.partition_broadcast(128), writes=["DBC"])
    LR, LI, DT = pt("LR"), pt("LI"), pt("DT")
    C.dma("sp", LR[1][:], lamre, writes=["LR"])
    C.dma("sp", LI[1][:], lamim, writes=["LI"])
    C.dma("sp", DT[1][:], logstep, writes=["DT"])
    C.dma("pool", WIN, win.rearrange("(k p) n -> p k n", p=128), writes=["WIN"])

    vact(DT, DT, AF.Exp)
    LD, MAG, M2, ANG = pt("LD"), pt("MAG"), pt("M2"), pt("ANG")
    v2(LD, LR, DT, ALU.mult)
    vact(MAG, LD, AF.Exp)
    vact(M2, LD, AF.Exp, scale=-2.0)
    v2(ANG, LI, DT, ALU.mult)
    CC, SS, T1, T2, T3 = pt("CC"), pt("SS"), pt("T1"), pt("T2"), pt("T3")
    HP = C.sb("HP", [128, 1], F32)
    C.memset("dve", HP[:], math.pi / 2, ["HP"])
    C.act(CC[1][:], ANG[1][:], AF.Sin, ["ANG", "HP"], ["CC"], bias=HP[:], scale=1.0 / 16)
    vact(SS, ANG, AF.Sin, scale=1.0 / 16)

    def csquare(X, Y):
        v2(T1, X, X, ALU.mult)
        v2(T2, Y, Y, ALU.mult)
        v2(T3, X, Y, ALU.mult)
        v2(X, T1, T2, ALU.subtract)
        vs(Y, T3, 2.0, ALU.mult)

    for _ in range(4):
        csquare(CC, SS)
    AR, AI = pt("AR"), pt("AI")
    v2(AR, MAG, CC, ALU.mult)
    v2(AI, MAG, SS, ALU.mult)
    NR, DEN, FRE, FIM = pt("NR"), pt("DEN"), pt("FRE"), pt("FIM")
    vs(NR, AR, -1.0, ALU.add)
    v2(T1, LR, LR, ALU.mult)
    v2(T2, LI, LI, ALU.mult)
    v2(DEN, T1, T2, ALU.add)
    C.S.op("dve", lambda e: e.reciprocal(out=DEN[1][:], in_=DEN[1][:]), reads=["DEN"], writes=["DEN"])
    v2(T1, NR, LR, ALU.mult)
    v2(T2, AI, LI, ALU.mult)
    v2(T1, T1, T2, ALU.add)
    v2(FRE, T1, DEN, ALU.mult)
    v2(T1, AI, LR, ALU.mult)
    v2(T2, NR, LI, ALU.mult)
    v2(T1, T1, T2, ALU.subtract)
    v2(FIM, T1, DEN, ALU.mult)
    AIS, FIS, AVR, AVIS = pt("AIS"), pt("FIS"), pt("AVR"), pt("AVIS")
    vs(AIS, AI, SG[:], ALU.mult)
    vs(FIS, FIM, SG[:], ALU.mult)
    v2(AVR, AR, M2, ALU.mult)
    v2(AVIS, AI, M2, ALU.mult)
    vs(AVIS, AVIS, NSG[:], ALU.mult)
    XR, XI = pt("XR"), pt("XI")
    C.copy("dve", XR[1][:], AR[1][:], ["AR"], ["XR"])
    C.copy("dve", XI[1][:], AI[1][:], ["AI"], ["XI"])
    for _ in range(4):
        csquare(XR, XI)
    A16IS = pt("A16IS")
    vs(A16IS, XI, SG[:], ALU.mult)
    A16R = XR
    YR, YI = pt("YR"), pt("YI")
    C.copy("dve", YR[1][:], XR[1][:], ["XR"], ["YR"])
    C.copy("dve", YI[1][:], XI[1][:], ["XI"], ["YI"])
    for _ in range(4):
        csquare(YR, YI)
    A256IS = pt("A256IS")
    vs(A256IS, YI, SG[:], ALU.mult)
    A256R = YR

    xT_v = xT.rearrange("(k p) t -> p k t", p=128)
    ev = 0
    for ct in range(2):
        C.dma("pool", XT, xT_v[:, :, ct * 2048:(ct + 1) * 2048], writes=["XT", "Zr"] + [("Z", g) for g in range(16)])
        for r in range(16):
            bank, bk = C.psum()
            for k in range(8):
                C.mm(bank[:, :], XT[:, k, r::16], WIN[:, k, :], k == 0, k == 7, ["XT", "WIN"], [bk])
            C.copy("act", U[:, ct, :, r, :], bank[:, :].rearrange("p (g h) -> p g h", g=32), [bk], [("U", ct)])
            ev += 1

    if after_uproj is not None:
        after_uproj()
    def tb(name, shape, dt=F32):
        return (name, C.sb(name, shape, dt))

    ARb, AISb, AVRb, AVISb, FRb, FISb = [tb(n, [128, 16, 16]) for n in ("ARb", "AISb", "AVRb", "AVISb", "FRb", "FISb")]
    SCR = C.sb("SCR", [128, 2, 2, 256], F32)
    CS = [("SCRa", SCR[:, 0, i, :].rearrange("p (g h) -> p g h", g=16)) for i in range(2)]
    CW = [("SCRb", SCR[:, 1, i, :].rearrange("p (g h) -> p g h", g=16)) for i in range(2)]
    BBS, BBW = tb("BBS", [128, 16, 16]), tb("BBW", [128, 16, 16])
    TA, TB = tb("TA", [128, 16, 16]), tb("TB", [128, 16, 16])
    PL = tb("PL", [128, 16, 16, 16], BF16)
    QR = tb("QR", [128, 16, 16, 16], BF16)
    INS = tb("INS", [128, 16, 16, 16], BF16)
    CA = tb("CA", [128, 16, 16, 16], BF16)
    IN = tb("IN", [128, 16, 2, 128], BF16)
    INW = tb("INW", [128, 16, 2, 128], BF16)
    TOEP = tb("TOEP", [128, 16, 2, 256], BF16)
    UT = tb("UT", [128, 16, 2, 256], BF16)
    SPB = tb("SPB", [128, 16, 256], BF16)
    RR, II = tb("RR", [128, 2, 16]), tb("II", [128, 2, 16])
    ZT1, ZT2 = tb("ZT1", [128, 2, 16]), tb("ZT2", [128, 2, 16])
    RR2, II2 = tb("RR2", [128, 2, 16]), tb("II2", [128, 2, 16])
    BT1 = ("SCRa", SCR[:, 0, :, :].rearrange("p a (g b) -> p a g b", g=16))
    BT2 = ("SCRb", SCR[:, 1, :, :].rearrange("p a (g b) -> p a g b", g=16))
    CCY = tb("CCY", [128, 2, 16, 16])
    YQ = [tb("YQ_0", [128, 16, 128])] * 2
    X2 = [tb("X2_0", [128, 16, 128])] * 2

    def bc(out, src, gh):
        C.copy("dve", out[1][:], src[1][:, gh * 16:(gh + 1) * 16].unsqueeze(2).broadcast_to([128, 16, 16]),
               [src[0]], [out[0]])

    def cmul(outS, outW, inS, inW, Rb, Isb):
        C.tt("dve", TA[1][:], Rb[1][:], inS[1], ALU.mult, [Rb[0], inS[0]], ["TA"])
        C.tt("dve", TB[1][:], Isb[1][:], inW[1], ALU.mult, [Isb[0], inW[0]], ["TB"])
        C.tt("dve", outS[1], TA[1][:], TB[1][:], ALU.add, ["TA", "TB"], [outS[0]])
        C.tt("dve", TA[1][:], Rb[1][:], inW[1], ALU.mult, [Rb[0], inW[0]], ["TA"])
        C.tt("dve", TB[1][:], Isb[1][:], inS[1], ALU.mult, [Isb[0], inS[0]], ["TB"])
        C.tt("dve", outW[1], TA[1][:], TB[1][:], ALU.subtract, ["TA", "TB"], [outW[0]])

    ycnt = {"n": 0}
    for gh in GH_ORDER:
        gsl = slice(gh * 16, (gh + 1) * 16)
        for o_, s_ in ((ARb, AR), (AISb, AIS), (AVRb, AVR), (AVISb, AVIS), (FRb, FRE), (FISb, FIS)):
            bc(o_, s_, gh)
        for tl, src, nm in ((BST, bst, "BST"), (BSW, bsw, "BSW"), (CST, cst, "CST"), (CSW, csw, "CSW")):
            C.dma("sp", tl[:], src[:, gsl, :], writes=[nm])
        cmul(("BBS", BBS[1][:]), ("BBW", BBW[1][:]), ("BST", BST[:]), ("BSW", BSW[:]), FRb, FISb)

        if dbg and gh == 1:
            C.dbg("ARb", ARb[1][:], [128, 16, 16], ["ARb"])
            C.dbg("AVISb", AVISb[1][:], [128, 16, 16], ["AVISb"])
            C.dbg("BST", BST[:], [128, 16, 16], ["BST"])
            C.dbg("BBS", BBS[1][:], [128, 16, 16], ["BBS"])
            C.dbg("AR", AR[1][:], [128, 32], ["AR"])
            C.dbg("FRE", FRE[1][:], [128, 32], ["FRE"])

        def run_steps(initS, initW, Rb, Isb, nsteps, emit):
            cur = (initS, initW)
            for k in range(nsteps):
                emit(k, cur[0])
                if k + 1 < nsteps:
                    nxt = (CS[k % 2], CW[k % 2])
                    cmul(nxt[0], nxt[1], cur[0], cur[1], Rb, Isb)
                    cur = nxt

        def emit_P(k, curS):
            C.ts("dve", PL[1][:, :, k, :], curS[1], NSG[:], ALU.mult, [curS[0], "NSG"], ["PL"])
        run_steps(("BBS", BBS[1][:]), ("BBW", BBW[1][:]), AVRb, AVISb, 16, emit_P)

        def emit_IN(k, curS):
            C.copy("dve", INS[1][:, :, 15 - k, :], curS[1], [curS[0]], ["INS"])
        run_steps(("BBS", BBS[1][:]), ("BBW", BBW[1][:]), ARb, AISb, 16, emit_IN)

        def emit_Q(k, curS):
            if k <= 15:
                C.copy("dve", QR[1][:, :, k, :], curS[1], [curS[0]], ["QR"])
            if k >= 1:
                C.ts("dve", CA[1][:, :, k - 1, :], curS[1], NSG[:], ALU.mult, [curS[0], "NSG"], ["CA"])
        run_steps(("CST", CST[:]), ("CSW", CSW[:]), ARb, AISb, 17, emit_Q)

        for g in range(16):
            bank, bk = C.psum()
            for kt in range(2):
                C.mm(bank[:, kt * 256:(kt + 1) * 256], PL[1][:, g, kt * 8:(kt + 1) * 8, :], QR[1][:, g, :, :], True, True,
                     ["PL", "QR"], [bk])
            C.tt("dve", TOEP[1][:, g, :, :], bank[:, :].rearrange("p (k n) -> p k n", k=2), CM[:], ALU.mult,
                 [bk, "CM"], [("TOEP", g)])
        if dbg and gh == 1:
            C.dbg("PL", PL[1][:], [128, 16, 16, 16], ["PL"])
            C.dbg("QR", QR[1][:], [128, 16, 16, 16], ["QR"])
            C.dbg("CA", CA[1][:], [128, 16, 16, 16], ["CA"])
            C.dbg("TOEP", TOEP[1][:], [128, 16, 2, 256], [("TOEP", g) for g in range(16)])
        for g2 in range(8):
            bank, bk = C.psum()
            tbf = bank[:, :].bitcast(BF16)
            for gi in range(2):
                g = g2 * 2 + gi
                for kt in range(2):
                    col = (gi * 2 + kt) * 128
                    C.tr(tbf[:, col:col + 128], INS[1][:, g, kt * 8:(kt + 1) * 8, :], IDB[:], ["INS", "IDB"], [bk])
            C.copy("act", IN[1][:, g2 * 2:g2 * 2 + 2, :, :], tbf[:, 0:512].rearrange("p (g k s) -> p g k s", g=2, k=2),
                   [bk], [("IN", g2)])
            for hf in range(2):
                C.copy("dve", INW[1][:, g2 * 2:g2 * 2 + 2, :, hf * 64:(hf + 1) * 64],
                       IN[1][:, g2 * 2:g2 * 2 + 2, :, (1 - hf) * 64:(2 - hf) * 64], [("IN", g2)], [("INW", g2)])
        for g in range(16):
            G = gh * 16 + g
            bank, bk = C.psum()
            tbf = bank[:, :].bitcast(BF16)
            for kt in range(2):
                for ct in range(2):
                    col = kt * 256 + ct * 128
                    C.tr(tbf[:, col:col + 128], U[:, ct, G, kt * 8:(kt + 1) * 8, :], IDB[:],
                         [("U", ct), "IDB"], [bk])
            C.copy("act" if g % 2 else "dve", UT[1][:, g, :, :], tbf[:, 0:512].rearrange("p (k c) -> p k c", k=2),
                   [bk], [("UT", g)])
        for g in range(16):
            bank, bk = C.psum()
            for sw in range(2):
                for kt in range(2):
                    lhsT = (INW if sw else IN)[1][:, g, kt, :]
                    C.mm(bank[:, sw * 256:(sw + 1) * 256], lhsT, UT[1][:, g, kt, :], kt == 0, kt == 1,
                         [("IN", g // 2), ("INW", g // 2), ("UT", g)], [bk])
            C.copy("act" if g % 2 else "dve", Z[:, :, g, :], bank[:, :].rearrange("p (s c) -> p s c", s=2),
                   [bk, "XT", "Zdone"], [("Z", g)])
        zall = [("Z", g) for g in range(16)]
        for s in range(2):
            C.copy("dve", RR[1][:, s, :], A16R[1][:, gsl], [A16R[0]], ["RR"])
        C.copy("dve", II[1][:, 0, :], A16IS[1][:, gsl], ["A16IS"], ["II"])
        C.ts("dve", II[1][:, 1, :], A16IS[1][:, gsl], -1.0, ALU.mult, ["A16IS"], ["II"])
        for s_ in range(2):
            C.copy("dve", RR2[1][:, s_, :], A256R[1][:, gsl], [A256R[0]], ["RR2"])
        C.copy("dve", II2[1][:, 0, :], A256IS[1][:, gsl], ["A256IS"], ["II2"])
        C.ts("dve", II2[1][:, 1, :], A256IS[1][:, gsl], -1.0, ALU.mult, ["A256IS"], ["II2"])
        Z5 = Z.rearrange("p s g (b j) -> p s g b j", j=16)
        RRb = RR[1][:].unsqueeze(3).broadcast_to([128, 2, 16, 16])
        IIb = II[1][:].unsqueeze(3).broadcast_to([128, 2, 16, 16])
        for j in range(1, 16):
            X = Z5[:, :, :, :, j - 1]
            C.tt("dve", BT1[1][:], RRb, X, ALU.mult, ["RR"] + (zall if j == 1 else ["Zr"]), ["SCRa"])
            C.tt("dve", BT2[1][:], IIb, X[:, ::-1], ALU.mult, ["II", "Zr"] + (zall if j == 1 else []), ["SCRb"])
            C.tt("dve", Z5[:, :, :, :, j], Z5[:, :, :, :, j], BT1[1][:], ALU.add, ["SCRa", "Zr"], ["Zr"])
            C.tt("dve", Z5[:, :, :, :, j], Z5[:, :, :, :, j], BT2[1][:], ALU.add, ["SCRb", "Zr"], ["Zr"])
        for b_ in range(1, 16):
            X = Z5[:, :, :, b_ - 1, 15]
            C.tt("dve", ZT1[1][:], RR2[1][:], X, ALU.mult, ["RR2", "Zr"], ["ZT1"])
            C.tt("dve", ZT2[1][:], II2[1][:], X[:, ::-1], ALU.mult, ["II2", "Zr"], ["ZT2"])
            C.tt("dve", Z5[:, :, :, b_, 15], Z5[:, :, :, b_, 15], ZT1[1][:], ALU.add, ["ZT1", "Zr"], ["Zr"])
            C.tt("dve", Z5[:, :, :, b_, 15], Z5[:, :, :, b_, 15], ZT2[1][:], ALU.add, ["ZT2", "Zr"], ["Zr"])
        CC = CCY[1][:, :, :, 0:15]
        C.copy("dve", CC, Z5[:, :, :, 0:15, 15], ["Zr"], ["CCY"])
        for j in range(15):
            C.tt("dve", BT1[1][:, :, :, 0:15], RRb[:, :, :, 0:15], CC, ALU.mult, ["RR", "CCY"], ["SCRa"])
            C.tt("dve", BT2[1][:, :, :, 0:15], IIb[:, :, :, 0:15], CC[:, ::-1], ALU.mult, ["II", "CCY"], ["SCRb"])
            C.tt("dve", CC, BT1[1][:, :, :, 0:15], BT2[1][:, :, :, 0:15], ALU.add, ["SCRa", "SCRb"], ["CCY"])
            C.tt("dve", Z5[:, :, :, 1:16, j], Z5[:, :, :, 1:16, j], CC, ALU.add, ["CCY", "Zr"], ["Zr"])
        if dbg and gh == 1:
            C.dbg("Z", Z, [128, 2, 16, 256], ["Zr"])
            C.dbg("UT", UT[1][:], [128, 16, 2, 256], [("UT", g) for g in range(16)])
            C.dbg("IN", IN[1][:], [128, 16, 2, 128], [("IN", g) for g in range(8)])
        C.memset("dve", SPB[1][:, :, 0:1], 0.0, ["SPB"])
        C.copy("dve", SPB[1][:, :, 1:256], Z[:, 0, :, 0:255], ["Zr"], ["SPB", "Zdone"])
        for ct in range(2):
            for ql in range(2):
                q = gh * 2 + ql
                i = ycnt["n"] % 2
                ycnt["n"] += 1
                C.tt("pool", YQ[i][1][:].rearrange("p r (g h) -> p r g h", g=8),
                     U[:, ct, q * 8:(q + 1) * 8, :, :].rearrange("p g r h -> p r g h"),
                     DBC[:, q * 128:(q + 1) * 128].rearrange("p (g h) -> p g h", g=8).unsqueeze(1).broadcast_to([128, 16, 8, 16]),
                     ALU.mult, [("U", ct), "DBC"], [YQ[i][0]])
                for j in range(4):
                    bank, bk = C.psum()
                    for gi in range(2):
                        g = ql * 8 + j * 2 + gi
                        o_ap = bank[:, gi * 256:(gi + 1) * 256]
                        for kt in range(2):
                            C.mm(o_ap, UT[1][:, g, kt, ct * 128:(ct + 1) * 128], TOEP[1][:, g, kt, :], kt == 0, False,
                                 [("UT", g), ("TOEP", g)], [bk])
                        C.mm(o_ap, SPB[1][:, g, ct * 128:(ct + 1) * 128], CA[1][:, g, :, :], False, True,
                             ["SPB", "CA"], [bk])
                    ysl = YQ[i][1][:, :, j * 32:(j + 1) * 32].rearrange("p r (g h) -> p g r h", g=2)
                    C.tt("dve", ysl, bank[:, :].rearrange("p (g r h) -> p g r h", g=2, r=16), ysl, ALU.add,
                         [bk, YQ[i][0]], [YQ[i][0]])
                C.act(X2[i][1][:], YQ[i][1][:], AF.Square, [YQ[i][0]], [X2[i][0]])
                C.ts("dve", X2[i][1][:], X2[i][1][:], 0.044715, ALU.mult, [X2[i][0]], [X2[i][0]], s2=1.0, op1=ALU.add)
                C.tt("pool", X2[i][1][:], X2[i][1][:], YQ[i][1][:], ALU.mult, [X2[i][0], YQ[i][0]], [X2[i][0]])
                C.act(X2[i][1][:], X2[i][1][:], AF.Sigmoid, [X2[i][0]], [X2[i][0]], scale=2.0 * math.sqrt(2.0 / math.pi))
                C.tt("pool", X2[i][1][:], X2[i][1][:], YQ[i][1][:], ALU.mult, [X2[i][0], YQ[i][0]], [X2[i][0]])
                if gT_out is None:
                    o = C.dma("sp", g_out.rearrange("(c r) ch -> c r ch", r=16)[ct * 128:(ct + 1) * 128, :, q * 128:(q + 1) * 128],
                              X2[i][1][:], reads=[X2[i][0]])
                    if final:
                        C.outs.append(o)
                else:
                    for r4 in range(4):
                        bank, bk = C.psum()
                        for rr in range(4):
                            C.tr(bank[:, rr * 128:(rr + 1) * 128], X2[i][1][:, r4 * 4 + rr, :], ID[:], [X2[i][0], "ID"], [bk])
                        C.copy("act" if r4 % 2 else "dve", GTS[:, :, r4 * 4:(r4 + 1) * 4].rearrange("p c r -> p r c"),
                               bank[:, :].rearrange("p (r c) -> p r c", r=4), [bk], ["GTS"])
                    C.dma("sp", gT_out[q * 128:(q + 1) * 128, ct * 2048:(ct + 1) * 2048],
                          WINB[:, :], reads=["GTS"], writes=[("gT_d", q, ct)])


_CACHE = {}


def _prog(name, fn):
    if name not in _CACHE:
        _CACHE[name] = fn()
    return _CACHE[name]


def _vec_layout(v, n):
    return np.ascontiguousarray(v.reshape(n, 128).T)


def _s5_inputs(x_b, w_in, lam_re, lam_im, b_re, b_im, c_re, c_im, d_skip, log_step, h):
    gs = slice(h * 32, (h + 1) * 32)
    rep = lambda a: np.ascontiguousarray(np.concatenate([a.T, a.T], axis=0))
    bre = b_re[gs].transpose(1, 0, 2)
    bim = b_im[gs].transpose(1, 0, 2)
    cre = c_re[gs].transpose(2, 0, 1)
    cim = c_im[gs].transpose(2, 0, 1)
    cm = np.zeros((128, 2, 256), np.float32)
    for kt in range(2):
        for rl in range(8):
            cm[rl * 16:(rl + 1) * 16, kt, (kt * 8 + rl) * 16:] = 1.0
    sign = np.concatenate([-np.ones((64, 1), np.float32), np.ones((64, 1), np.float32)], 0)
    return {"xT": np.ascontiguousarray(x_b.T), "win": np.ascontiguousarray(w_in[:, h * 512:(h + 1) * 512]),
            "lamre": rep(lam_re[gs]), "lamim": rep(lam_im[gs]),
            "logstep": np.ascontiguousarray(np.broadcast_to(log_step[gs][None, :], (128, 32))),
            "bst": np.ascontiguousarray(np.concatenate([bre, bim], 0)),
            "bsw": np.ascontiguousarray(np.concatenate([bim, bre], 0)),
            "cst": np.ascontiguousarray(np.concatenate([cre, cim], 0)),
            "csw": np.ascontiguousarray(np.concatenate([cim, cre], 0)),
            "dvec": np.ascontiguousarray(d_skip[h * 512:(h + 1) * 512][None, :]),
            "ident": np.eye(128, dtype=np.float32), "cmask": cm, "sign": sign}


def build_fused():
    C = Ctx()
    xT = C.din("xT", [D, SEQ])
    x = C.din("x", [SEQ, D])
    win = C.din("win", [D, D])
    lamre = C.din("lamre", [128, 64])
    lamim = C.din("lamim", [128, 64])
    logstep = C.din("logstep", [128, 64])
    bst = C.din("bst", [128, 64, 16])
    bsw = C.din("bsw", [128, 64, 16])
    cst = C.din("cst", [128, 64, 16])
    csw = C.din("csw", [128, 64, 16])
    dvec = C.din("dvec", [1, D])
    ident = C.din("ident", [128, 128])
    cmask = C.din("cmask", [128, 2, 256])
    sign = C.din("sign", [128, 1])
    masku = C.din("masku", [128, 128])
    wglu = C.din("wglu", [D, D])
    bglu = C.din("bglu", [128, 8])
    wout0 = C.din("wout0", [D, D])
    wout1 = C.din("wout1", [D, D])
    w1 = [C.din(f"w1_{l}", [D, DFF]) for l in range(2)]
    b1 = [C.din(f"b1_{l}", [128, 32]) for l in range(2)]
    w2 = [C.din(f"w2_{l}", [DFF, D]) for l in range(2)]
    vecs = [C.din(f"vecs_{l}", [5, D]) for l in range(2)]
    wq = C.din("wq", [D, D])
    wkv = C.din("wkv", [D, 2 * D])
    kidx = C.din("kidx", [128, 32], U32)
    qidx = C.din("qidx", [128, 16], U32)
    mask2 = C.din("mask2", [128, 256])
    out = C.dout("out", [SEQ // 2, D])
    gT_d = C.dscratch("gT_d", [D, SEQ])
    x1_d = C.dscratch("x1_d", [SEQ, D])
    x1own_d = C.dscratch("x1own_d", [SEQ // 2, D])
    oT_d = C.dscratch("oT_d", [D, SEQ // 2])
    C.init_psum()

    def bf16_copy(name, src, rows, cols):
        dst = C.dscratch(name + "_bf", [rows, cols], BF16)
        def issue():
            for r0 in range(0, rows, 1024):
                for c0 in range(0, cols, 2048):
                    c1 = min(cols, c0 + 2048)
                    C.dma("pool", dst[r0:r0 + 1024, c0:c1], src[r0:r0 + 1024, c0:c1], writes=[(name, r0, c0)])
        return dst, issue
    wglu, i1 = bf16_copy("wglu", wglu, D, D)
    wout0, i2 = bf16_copy("wout0", wout0, D, D)
    w1[0], i3 = bf16_copy("w1_0", w1[0], D, DFF)
    w2[0], i4 = bf16_copy("w2_0", w2[0], DFF, D)
    wq, i5 = bf16_copy("wq", wq, D, D)
    wkv, i6 = bf16_copy("wkv", wkv, D, 2 * D)
    wout1, i7 = bf16_copy("wout1", wout1, D, D)
    w1[1], i8 = bf16_copy("w1_1", w1[1], D, DFF)
    w2[1], i9 = bf16_copy("w2_1", w2[1], DFF, D)
    cast_batches = [lambda: [f_() for f_ in (i1, i2, i3, i4)], lambda: [f_() for f_ in (i5, i6, i7, i8, i9)]]
    for hh in range(2):
        gs = slice(hh * 32, (hh + 1) * 32)
        cs = slice(hh * 512, (hh + 1) * 512)
        T = {"xT": xT, "win": win[:, cs], "lamre": lamre[:, gs], "lamim": lamim[:, gs], "logstep": logstep[:, gs],
             "bst": bst[:, gs, :], "bsw": bsw[:, gs, :], "cst": cst[:, gs, :], "csw": csw[:, gs, :],
             "dvec": dvec[:, cs], "ident": ident, "cmask": cmask, "sign": sign}
        emit_s5(C, T, final=False, gT_out=gT_d[cs, :], after_uproj=cast_batches[hh])
        C.next_stage()
    TB = {"inT": gT_d, "xres": x, "wglu": wglu, "bglu": bglu, "wout": wout0, "w1": w1[0], "b1": b1[0], "w2": w2[0],
          "vecs": vecs[0], "ident": ident, "xout": x1_d}
    emit_post(C, TB, True, SEQ, 1024, final=False)
    C.next_stage()
    TC = {"x1": x1_d, "x1own": x1own_d, "kidx": kidx, "qidx": qidx, "mask2": mask2, "wq": wq, "wk": wkv[:, 0:D], "wv": wkv[:, D:2 * D],
          "ident": ident, "masku": masku, "oT": oT_d}
    emit_attn(C, TC, 8, final=False, half=True)
    C.next_stage()
    TD = {"inT": oT_d, "xres": x1own_d, "wout": wout1, "w1": w1[1], "b1": b1[1], "w2": w2[1], "vecs": vecs[1],
          "ident": ident, "xout": out}
    emit_post(C, TD, False, SEQ // 2, 1024, final=True)
    return C.finish()


def kernel(x, s5_w_in, s5_lambda_re, s5_lambda_im, s5_b_re, s5_b_im, s5_c_re, s5_c_im,
           s5_d, s5_log_step, s5_w_glu, s5_b_glu, s5_w_out,
           sb_w_kv, sb_w_q, sb_w_out,
           mlp_w1, mlp_b1, mlp_w2, mlp_b2,
           ln_mix_g, ln_mix_b, ln_mlp_g, ln_mlp_b):
    f = lambda a: np.ascontiguousarray(np.asarray(a, dtype=np.float32))
    x = f(x)
    rep = lambda a: np.ascontiguousarray(np.concatenate([a.T, a.T], axis=0))
    bre = f(s5_b_re)[0].transpose(1, 0, 2)
    bim = f(s5_b_im)[0].transpose(1, 0, 2)
    cre = f(s5_c_re)[0].transpose(2, 0, 1)
    cim = f(s5_c_im)[0].transpose(2, 0, 1)
    cm = np.zeros((128, 2, 256), np.float32)
    for kt in range(2):
        for rl in range(8):
            cm[rl * 16:(rl + 1) * 16, kt, (kt * 8 + rl) * 16:] = 1.0
    shared = {
        "win": f(s5_w_in)[0], "lamre": rep(f(s5_lambda_re)[0]), "lamim": rep(f(s5_lambda_im)[0]),
        "logstep": np.ascontiguousarray(np.broadcast_to(f(s5_log_step)[0][None, :], (128, 64))),
        "bst": np.ascontiguousarray(np.concatenate([bre, bim], 0)), "bsw": np.ascontiguousarray(np.concatenate([bim, bre], 0)),
        "cst": np.ascontiguousarray(np.concatenate([cre, cim], 0)), "csw": np.ascontiguousarray(np.concatenate([cim, cre], 0)),
        "dvec": f(s5_d)[0][None, :].copy(), "ident": np.eye(128, dtype=np.float32), "cmask": cm,
        "sign": np.concatenate([-np.ones((64, 1), np.float32), np.ones((64, 1), np.float32)], 0),
        "masku": np.triu(np.ones((128, 128), np.float32)),
        "wglu": f(s5_w_glu)[0], "bglu": _vec_layout(f(s5_b_glu)[0], 8), "wout0": f(s5_w_out)[0], "wout1": f(sb_w_out)[0],
        "wq": f(sb_w_q)[0], "wkv": f(sb_w_kv),
    }
    for l in range(2):
        shared[f"w1_{l}"] = f(mlp_w1)[l]
        shared[f"b1_{l}"] = _vec_layout(f(mlp_b1)[l], 32)
        shared[f"w2_{l}"] = f(mlp_w2)[l]
        shared[f"vecs_{l}"] = np.stack([f(ln_mix_g)[l], f(ln_mix_b)[l], f(mlp_b2)[l], f(ln_mlp_g)[l], f(ln_mlp_b)[l]])
    in_maps = []
    ar = np.arange(128, dtype=np.uint32)[:, None]
    tiles = np.arange(32, dtype=np.uint32)[None, :]
    for c in range(8):
        b, h = c // 2, c % 2
        m = dict(shared)
        m["xT"] = np.ascontiguousarray(x[b].T)
        m["x"] = x[b]
        m["kidx"] = np.ascontiguousarray((tiles * 128 + ar).astype(np.uint32))
        qblocks = 2 * np.arange(16, dtype=np.uint32)[None, :] + h
        m["qidx"] = np.ascontiguousarray((qblocks * 128 + ar).astype(np.uint32))
        tri = np.triu(np.ones((128, 128), np.float32))
        if h == 1:
            m["mask2"] = np.ascontiguousarray(np.concatenate([np.zeros((128, 128), np.float32), tri], 1))
        else:
            m["mask2"] = np.ascontiguousarray(np.concatenate([tri, np.ones((128, 128), np.float32)], 1))
        in_maps.append(m)
    res = run_bass_kernel_spmd(_prog("fused", build_fused), in_maps, core_ids=list(range(8)))
    out = np.empty((BATCH, SEQ, D), np.float32)
    for c in range(8):
        b, h = c // 2, c % 2
        r = res.results[c]["out"].reshape(16, 128, D)
        out[b].reshape(32, 128, D)[h::2] = r
    return out
```
